# Optimizing a Trainium2 kernel written in Bass

```python
import math
import jax, jax.numpy as jnp
from jax import lax
import numpy as np

D_MODEL = 1024
BATCH = 8
SEQ = 2048
DEPTH = 1
DEC_BATCH = 128
DEC_SEQ = 4
PAST_LEN = 16384
PAGE_SIZE = 128

D_CONV = D_MODEL
CONV_W = 3
N_HEADS = 4
D_HEAD = D_MODEL // N_HEADS
D_MLSTM = N_HEADS * D_HEAD
D_FF = 4 * D_MODEL
CHUNK = 128
LN_EPS = 1e-5
ALPHA = (2.0 * DEPTH) ** 0.25
BETA = (8.0 * DEPTH) ** -0.25
F_BIAS_LO = 3.0
F_BIAS_HI = 6.0
D_IN = 3 * D_CONV + 4 * D_MLSTM + 2 * N_HEADS + 2 * D_MODEL

kernel_name = "hybrid_shortconv_mlstm_gated_merge_step"


def _split_points():
    sizes = (D_CONV, D_CONV, D_CONV, D_MLSTM, D_MLSTM, D_MLSTM, D_MLSTM,
             N_HEADS, N_HEADS, D_MODEL, D_MODEL)
    pts, acc = [], 0
    for s in sizes[:-1]:
        acc += s
        pts.append(acc)
    return pts


def layer_norm(x, g, b):
    xf = x.astype(jnp.float32)
    mu = jnp.mean(xf, axis=-1, keepdims=True)
    xc = xf - mu
    var = jnp.mean(xc * xc, axis=-1, keepdims=True)
    y = xc * lax.rsqrt(var + LN_EPS) * g.astype(jnp.float32) + b.astype(jnp.float32)
    return y.astype(x.dtype)


def short_conv(u, buf, w):
    T = u.shape[1]
    up = jnp.concatenate([buf.astype(u.dtype), u], axis=1)
    y = up[:, 0:T] * w[0]
    for j in range(1, CONV_W):
        y = y + up[:, j:j + T] * w[j]
    return y, up[:, -(CONV_W - 1):]


def mlstm_chunkwise(q, k, v, logi, logf, C0, n0, m0, chunk):
    Bsz, T, H, D = q.shape
    nc = T // chunk
    L = chunk

    def to_chunks(a):
        return a.reshape(Bsz, nc, L, H, D).transpose(1, 0, 3, 2, 4)

    def g_chunks(a):
        return a.reshape(Bsz, nc, L, H).transpose(1, 0, 3, 2)

    mask = jnp.tril(jnp.ones((L, L), dtype=bool))

    def step(carry, xs):
        C, n, m = carry
        qc, kc, vc, ic, fc = xs
        b = jnp.cumsum(fc, axis=-1)
        inter = b + m[..., None]
        Dm = b[..., :, None] - b[..., None, :] + ic[..., None, :]
        Dm = jnp.where(mask, Dm, -jnp.inf)
        m_t = jnp.maximum(inter, jnp.max(Dm, axis=-1))
        w_inter = jnp.exp(inter - m_t)
        S = jnp.einsum('bhtd,bhsd->bhts', qc, kc) * jnp.exp(Dm - m_t[..., None])
        num = (w_inter[..., None] * jnp.einsum('bhtd,bhdv->bhtv', qc, C)
               + jnp.einsum('bhts,bhsv->bhtv', S, vc))
        den = w_inter * jnp.einsum('bhtd,bhd->bht', qc, n) + jnp.sum(S, axis=-1)
        h = num / jnp.maximum(jnp.abs(den), jnp.exp(-m_t))[..., None]
        m_new = m_t[..., -1]
        decay = jnp.exp(b[..., -1] + m - m_new)
        ws = jnp.exp(ic + b[..., -1:] - b - m_new[..., None])
        C_new = decay[..., None, None] * C + jnp.einsum('bhs,bhsd,bhsv->bhdv', ws, kc, vc)
        n_new = decay[..., None] * n + jnp.einsum('bhs,bhsd->bhd', ws, kc)
        return (C_new, n_new, m_new), h

    (C, n, m), hs = lax.scan(step, (C0, n0, m0),
                             (to_chunks(q), to_chunks(k), to_chunks(v), g_chunks(logi), g_chunks(logf)))
    h = hs.transpose(1, 0, 3, 2, 4).reshape(Bsz, T, H, D)
    return h, C, n, m


def hybrid_layer(x, conv_buf, C0, n0, m0, chunk, w_in, b_gate, conv_w, w_conv_out, mh_g,
                 w_m_out, w_o, ln1_g, ln1_b, w_ff1, w_ff2, ln2_g, ln2_b):
    Bsz, T, _ = x.shape
    f32 = jnp.float32
    z = x @ w_in
    (bg, cg, hc, q, k, v, o, ig, fg, gc, gm) = jnp.split(z, _split_points(), axis=-1)
    conv, new_buf = short_conv(cg * hc, conv_buf, conv_w)
    y_conv = (bg * conv) @ w_conv_out
    qf = q.reshape(Bsz, T, N_HEADS, D_HEAD).astype(f32)
    kf = k.reshape(Bsz, T, N_HEADS, D_HEAD).astype(f32) * (D_HEAD ** -0.5)
    vf = v.reshape(Bsz, T, N_HEADS, D_HEAD).astype(f32)
    bgf = b_gate.astype(f32)
    logi = ig.astype(f32) + bgf[:N_HEADS]
    logf = jax.nn.log_sigmoid(fg.astype(f32) + bgf[N_HEADS:])
    h, C, n, m = mlstm_chunkwise(qf, kf, vf, logi, logf, C0.astype(f32), n0.astype(f32),
                                 m0.astype(f32), chunk)
    mu = jnp.mean(h, axis=-1, keepdims=True)
    hc_ = h - mu
    h = hc_ * lax.rsqrt(jnp.mean(hc_ * hc_, axis=-1, keepdims=True) + LN_EPS)
    h = h.reshape(Bsz, T, D_MLSTM) * mh_g.astype(f32) * jax.nn.sigmoid(o.astype(f32))
    y_m = h.astype(x.dtype) @ w_m_out
    merged = jax.nn.sigmoid(gc) * y_conv + jax.nn.sigmoid(gm) * y_m
    x1 = layer_norm(ALPHA * x + merged @ w_o, ln1_g, ln1_b)
    hid = jnp.square(jax.nn.relu(x1 @ w_ff1))
    x2 = layer_norm(ALPHA * x1 + hid @ w_ff2, ln2_g, ln2_b)
    dt = x.dtype
    return x2, new_buf.astype(dt), C.astype(dt), n.astype(dt), m.astype(dt)


def setup_inputs(seed: int = 0) -> dict:
    key = jax.random.key(seed)
    ks = jax.random.split(key, 24)
    nrm = lambda kk, shape, s: jax.random.normal(kk, shape, jnp.float32) * s
    f_bias = jnp.linspace(F_BIAS_LO, F_BIAS_HI, N_HEADS, dtype=jnp.float32)
    b_gate = jnp.concatenate([
        nrm(ks[9], (DEPTH, N_HEADS), 0.1),
        f_bias[None, :] + nrm(ks[10], (DEPTH, N_HEADS), 0.1)], axis=-1)
    return {
        "x_prompt": nrm(ks[0], (BATCH, SEQ, D_MODEL), 1.0),
        "x_sample": nrm(ks[1], (DEC_BATCH, DEC_SEQ, D_MODEL), 1.0),
        "state_conv": nrm(ks[2], (DEPTH, DEC_BATCH, CONV_W - 1, D_CONV), 1.0),
        "state_C": nrm(ks[3], (DEPTH, DEC_BATCH, N_HEADS, D_HEAD, D_HEAD), 0.1),
        "state_n": nrm(ks[4], (DEPTH, DEC_BATCH, N_HEADS, D_HEAD), 0.1),
        "state_m": 1.0 + nrm(ks[5], (DEPTH, DEC_BATCH, N_HEADS), 0.5),
        "w_in": nrm(ks[6], (DEPTH, D_MODEL, D_IN), D_MODEL ** -0.5),
        "b_gate": b_gate,
        "conv_w": nrm(ks[7], (DEPTH, CONV_W, D_CONV), CONV_W ** -0.5),
        "w_conv_out": nrm(ks[8], (DEPTH, D_CONV, D_MODEL), BETA * D_CONV ** -0.5),
        "mh_g": 1.0 + nrm(ks[11], (DEPTH, D_MLSTM), 0.02),
        "w_m_out": nrm(ks[12], (DEPTH, D_MLSTM, D_MODEL), BETA * D_MLSTM ** -0.5),
        "w_o": nrm(ks[13], (DEPTH, D_MODEL, D_MODEL), BETA * D_MODEL ** -0.5),
        "ln1_g": 1.0 + nrm(ks[14], (DEPTH, D_MODEL), 0.02),
        "ln1_b": nrm(ks[15], (DEPTH, D_MODEL), 0.02),
        "w_ff1": nrm(ks[16], (DEPTH, D_MODEL, D_FF), BETA * D_MODEL ** -0.5),
        "w_ff2": nrm(ks[17], (DEPTH, D_FF, D_MODEL), BETA * D_FF ** -0.5),
        "ln2_g": 1.0 + nrm(ks[18], (DEPTH, D_MODEL), 0.02),
        "ln2_b": nrm(ks[19], (DEPTH, D_MODEL), 0.02),
    }


def reference(x_prompt, x_sample, state_conv, state_C, state_n, state_m, w_in, b_gate, conv_w,
              w_conv_out, mh_g, w_m_out, w_o, ln1_g, ln1_b, w_ff1, w_ff2, ln2_g, ln2_b):
    dt = x_prompt.dtype
    chunk_p = CHUNK if SEQ % CHUNK == 0 else SEQ
    xp, xs = x_prompt, x_sample
    cp_l, cs_l, Cp_l, Cs_l, np_l, ns_l, mp_l, ms_l = [], [], [], [], [], [], [], []
    for l in range(DEPTH):
        params = (w_in[l], b_gate[l], conv_w[l], w_conv_out[l], mh_g[l], w_m_out[l], w_o[l],
                  ln1_g[l], ln1_b[l], w_ff1[l], w_ff2[l], ln2_g[l], ln2_b[l])
        xp, cp, Cp, np_, mp = hybrid_layer(
            xp, jnp.zeros((BATCH, CONV_W - 1, D_CONV), dt),
            jnp.zeros((BATCH, N_HEADS, D_HEAD, D_HEAD), jnp.float32),
            jnp.zeros((BATCH, N_HEADS, D_HEAD), jnp.float32),
            jnp.zeros((BATCH, N_HEADS), jnp.float32), chunk_p, *params)
        xs, cs, Cs, ns, ms = hybrid_layer(
            xs, state_conv[l], state_C[l], state_n[l], state_m[l], DEC_SEQ, *params)
        cp_l.append(cp); cs_l.append(cs); Cp_l.append(Cp); Cs_l.append(Cs)
        np_l.append(np_); ns_l.append(ns); mp_l.append(mp); ms_l.append(ms)
    return (xp, xs, jnp.stack(cp_l), jnp.stack(cs_l), jnp.stack(Cp_l), jnp.stack(Cs_l),
            jnp.stack(np_l), jnp.stack(ns_l), jnp.stack(mp_l), jnp.stack(ms_l))
```

```python
import os
from contextlib import ExitStack
import numpy as np
import concourse.bass as bass
import concourse.mybir as mybir
from concourse.bass_utils import run_bass_kernel_spmd

F32 = mybir.dt.float32
BF16 = mybir.dt.bfloat16
AF = mybir.ActivationFunctionType
ALU = mybir.AluOpType
AX = mybir.AxisListType

D = 1024
SEQ = 2048
NCORE = 8
NSEQ = 16
DEC = 4
NH = 4
DH = 256
LN_EPS = 1e-5
ALPHA = 2.0 ** 0.25
NEG = -1.0e4
KSCALE = DH ** -0.5

SAME_ENGINE_SYNC = True
SAME_ENGINE_RAW_ONLY = False
LIST_SCHED = True
SCHED_XLAT = 0.5
ACC_ENG = 'scalar'
HB_MODE = 'pool'
LN_FUSE = True
ST_ENG = 'sync'
CS_ST_ENG = 'scalar'
FF_GRP = 512
SCHED_EPS = 0.3
LN2B_ENG = ('gpsimd',)
LN1B_ENG = ('gpsimd',)
MASK_MM = False
CBF_ENG = 'vector'
VTOK_ENG = 'scalar'
NPM = 2
KTOK_TRANSPOSE = False
NB3 = 3
NYT = 3
LN2G_ENG = 'vector'
GATES_OVERLAP = True
SQ_ENG = ('gpsimd',)
NHID = 2
LNB_ENG = 'gpsimd'
X1T_ENG = 'scalar'
NC0X = 7
SCHED_WINDOW = 400
SCHED_SLAT = 0.15
CAST_PATTERN = ("scalar", "scalar", "gpsimd", "scalar", "scalar", "vector", "scalar")


class _Op:
    __slots__ = ("eng", "fns", "reads", "writes", "deps", "is_dma", "semkey", "idx", "stage", "raw", "t_start", "t_fin", "crit", "t_ready",
                 "need_inc", "inc_val", "barrier", "is_barrier", "b_last", "b_dma")

    def __init__(self, eng, fns, reads, writes, is_dma, semkey):
        self.eng = eng
        self.fns = fns
        self.reads = reads
        self.writes = writes
        self.deps = set()
        self.is_dma = is_dma
        self.semkey = semkey
        self.need_inc = False
        self.inc_val = None
        self.barrier = None
        self.is_barrier = False


class Prog:
    ENGS = ("sync", "scalar", "vector", "gpsimd", "tensor")

    def __init__(self, nc):
        self.nc = nc
        self.ops = []
        self.last_writer = {}
        self.readers = {}
        self.barrier_op = None
        self.enabled = True
        self.stage = "init"

    def _record(self, op):
        op.stage = self.stage
        if not self.enabled:
            return op
        pr = tuple(r for r in op.reads if isinstance(r, tuple) and r and r[0] == "pb")
        if pr:
            op.reads = tuple(r for r in op.reads if r not in pr)
            op.writes = tuple(op.writes) + tuple(r for r in pr if r not in op.writes)
        op.idx = len(self.ops)
        deps = set()
        for r in op.reads:
            w = self.last_writer.get(r)
            if w is not None:
                deps.add(w)
        op.raw = set(deps)
        for r in op.writes:
            w = self.last_writer.get(r)
            if w is not None:
                deps.add(w)
            for rd in self.readers.get(r, ()):
                deps.add(rd)
        deps.discard(op.idx)
        op.deps = deps
        op.barrier = self.barrier_op
        for r in op.writes:
            self.last_writer[r] = op.idx
            self.readers[r] = []
        for r in op.reads:
            if r not in op.writes:
                self.readers.setdefault(r, []).append(op.idx)
        self.ops.append(op)
        return op

    def op(self, eng, fns, reads=(), writes=()):
        if isinstance(fns, tuple):
            fns = [fns]
        return self._record(_Op(eng, list(fns), tuple(reads), tuple(writes), False, None))

    def dma(self, eng, fn, reads=(), writes=(), semkey=None):
        assert semkey is not None
        return self._record(_Op(eng, [fn], tuple(reads), tuple(writes), True, semkey))

    def barrier(self):
        if not self.enabled:
            return
        b = _Op("sync", [], (), (), False, None)
        b.is_barrier = True
        b.raw = set()
        self._record(b)
        self.last_writer = {}
        self.readers = {}
        self.barrier_op = b.idx

    def _skip_same(self, dop, o):
        if dop.eng == o.eng and not o.is_dma and not dop.is_dma:
            if dop.eng == "tensor" or not SAME_ENGINE_SYNC:
                return True
            if SAME_ENGINE_RAW_ONLY and dop.idx not in o.raw:
                return True
        return False


    @staticmethod
    def _free(ap):
        sh = ap.shape
        n = 1
        for d in sh[1:]:
            n *= int(d)
        return n, int(sh[0])

    def _cost(self, o):
        if o.is_dma:
            meth, args, kw = o.fns[0]
            out = kw["out"]
            n, p = self._free(out)
            esz = 2 if out.dtype == BF16 else 4
            return 0.15, 2.0 + n * p * esz / 300e3
        t = 0.0
        for meth, args, kw in o.fns:
            out = kw.get("out", args[0] if args else None)
            n, p = self._free(out)
            if o.eng == "tensor":
                if meth == "matmul":
                    c = max(n, 64) / 2300.0
                    if kw["lhsT"].dtype == F32:
                        c *= 4
                    t += c + 0.01
                else:
                    t += 0.12
            elif o.eng == "scalar":
                t += 0.12 + n / 1000.0
            elif o.eng == "vector":
                t += 0.12 + n / 960.0
            else:
                t += 0.3 + n * 2.3 / 1000.0
        return t, 0.0

    def schedule(self):
        ops = self.ops
        XLAT = SCHED_XLAT
        SLAT = SCHED_SLAT
        order = []
        t_eng = {e: 0.0 for e in self.ENGS}
        dma_pipe = [0.0]
        fin = {}
        regions = []
        cur = []
        for o in ops:
            if o.is_barrier:
                regions.append((cur, o))
                cur = []
            else:
                cur.append(o)
        regions.append((cur, None))
        for reg, bar in regions:
            if reg:
                ids = set(o.idx for o in reg)
                cost = {o.idx: self._cost(o) for o in reg}
                succ = {o.idx: [] for o in reg}
                npred = {}
                for o in reg:
                    ds = [d for d in o.deps if d in ids]
                    npred[o.idx] = len(ds)
                    for d in ds:
                        succ[d].append(o.idx)
                bl = {}
                for o in reversed(reg):
                    m = 0.0
                    for s_ in succ[o.idx]:
                        m = max(m, bl[s_] + (XLAT if ops[s_].eng != o.eng else SLAT))
                    bl[o.idx] = m + cost[o.idx][0] + cost[o.idx][1]
                t0 = max(t_eng.values())
                for e in t_eng:
                    t_eng[e] = t0
                _busy = {e: 0.0 for e in self.ENGS}
                crit = {}
                ready = {e: [] for e in self.ENGS}
                rtime = {}
                for o in reg:
                    if npred[o.idx] == 0:
                        ready[o.eng].append(o.idx)
                        rtime[o.idx] = t0
                nleft = len(reg)
                WINDOW = SCHED_WINDOW
                base_pos = reg[0].idx
                done_cnt = 0
                sched_flags = {}
                lowest_unsched = reg[0].idx
                reg_idx = [o.idx for o in reg]
                ptr = 0
                while nleft:
                    while ptr < len(reg_idx) and reg_idx[ptr] in sched_flags:
                        ptr += 1
                    low = reg_idx[ptr] if ptr < len(reg_idx) else None
                    best = None
                    for e in self.ENGS:
                        if not ready[e]:
                            continue
                        cands = [i for i in ready[e] if low is None or i - low <= WINDOW]
                        if not cands:
                            continue
                        est = [(max(t_eng[e], rtime[i]), i) for i in cands]
                        mn = min(x[0] for x in est)
                        near = [i for (st, i) in est if st <= mn + SCHED_EPS]
                        pick = max(near, key=lambda i: (bl[i], -i))
                        key = (mn, -bl[pick], pick)
                        if best is None or key < best[0]:
                            best = (key, e, pick)
                    if best is None:
                        allc = [(i, e) for e in self.ENGS for i in ready[e]]
                        i, e = min(allc)
                        best = ((max(t_eng[e], rtime[i]), 0, i), e, i)
                    (_, e, i) = best
                    o = ops[i]
                    st = max(t_eng[e], rtime[i])
                    occ, lat = cost[i]
                    if o.is_dma:
                        t_eng[e] = st + occ
                        s2 = max(st, dma_pipe[0])
                        f = s2 + lat
                        dma_pipe[0] = s2 + (lat - 2.0)
                    else:
                        t_eng[e] = st + occ
                        f = st + occ
                    fin[i] = f
                    o.t_start = st
                    o.t_fin = f
                    o.crit = crit.get(i)
                    o.t_ready = rtime[i]
                    _busy[e] += occ
                    ready[e].remove(i)
                    sched_flags[i] = True
                    order.append(o)
                    nleft -= 1
                    for s_ in succ[i]:
                        npred[s_] -= 1
                        lt = XLAT if ops[s_].eng != e or o.is_dma else SLAT
                        if f + lt > rtime.get(s_, t0):
                            crit[s_] = i
                        rtime[s_] = max(rtime.get(s_, t0), f + lt)
                        if npred[s_] == 0:
                            ready[ops[s_].eng].append(s_)
            if reg:
                if not hasattr(self, "report"):
                    self.report = []
                self.report.append((reg[0].stage, reg[-1].stage, t0, max(max(t_eng.values()), max(fin[o.idx] for o in reg)), dict(_busy), len(reg)))
            if bar is not None:
                order.append(bar)
        self.est_time = max(t_eng.values())
        return order

    def emit(self, sem_ctx):
        nc = self.nc
        ops = self.ops
        order = self.schedule() if LIST_SCHED else list(ops)
        self.order = order
        last_c = {}
        for o in order:
            if o.is_barrier:
                o.b_last = dict(last_c)
                for e, i in last_c.items():
                    ops[i].need_inc = True
                continue
            for d in o.deps:
                dop = ops[d]
                if dop.is_dma or self._skip_same(dop, o):
                    continue
                dop.need_inc = True
            if not o.is_dma and o.fns:
                last_c[o.eng] = o.idx
        eng_cnt = {e: 0 for e in self.ENGS}
        dma_cnt = {}
        for o in order:
            if o.is_dma:
                dma_cnt[o.semkey] = dma_cnt.get(o.semkey, 0) + 1
            elif o.need_inc:
                eng_cnt[o.eng] += 1
                o.inc_val = eng_cnt[o.eng]
        eng_sem = {e: sem_ctx("e_" + e) for e in self.ENGS if eng_cnt[e] > 0}
        dma_sem = {k: sem_ctx("d_%d" % i) for i, k in enumerate(dma_cnt)}
        self.n_sems = len(eng_sem) + len(dma_sem)
        per_eng = {e: [] for e in self.ENGS}
        running = {}
        snapshots = {}
        for o in order:
            if o.is_barrier:
                o.b_dma = dict(running)
                continue
            if o.is_dma:
                running[o.semkey] = running.get(o.semkey, 0) + 1
            per_eng[o.eng].append(o)
            need = {}
            if o.barrier is not None:
                b = ops[o.barrier]
                for e, i in b.b_last.items():
                    if e == "tensor" and o.eng == "tensor" and not o.is_dma:
                        continue
                    need[("e", e)] = ops[i].inc_val
                for k, c in b.b_dma.items():
                    need[("d", k)] = 16 * c
            for d in o.deps:
                dop = ops[d]
                if dop.is_dma:
                    cnt = running[dop.semkey]
                    if o.is_dma and o.semkey == dop.semkey:
                        cnt -= 1
                    key = ("d", dop.semkey)
                    need[key] = max(need.get(key, 0), 16 * cnt)
                else:
                    if self._skip_same(dop, o):
                        continue
                    key = ("e", dop.eng)
                    need[key] = max(need.get(key, 0), dop.inc_val)
            snapshots[o.idx] = need
        final_dma = dict(running)

        def run_engine(ename, engobj, final_waiter):
            waited = {}
            for o in per_eng[ename]:
                for key, val in snapshots[o.idx].items():
                    if val <= 0 or waited.get(key, 0) >= val:
                        continue
                    sem = dma_sem[key[1]] if key[0] == "d" else eng_sem[key[1]]
                    if os.environ.get("PROG_DEBUG"):
                        print("WAIT", ename, o.idx, [f[0] for f in o.fns][:1], key, val)
                    engobj.wait_ge(sem, val)
                    waited[key] = val
                n = len(o.fns)
                for i, fn in enumerate(o.fns):
                    ins = getattr(engobj, fn[0])(*fn[1], **fn[2])
                    if i == n - 1:
                        if o.is_dma:
                            ins.then_inc(dma_sem[o.semkey], 16)
                        elif o.need_inc:
                            ins.then_inc(eng_sem[o.eng], 1)
            if final_waiter:
                for k, cnt in final_dma.items():
                    if waited.get(("d", k), 0) < 16 * cnt:
                        engobj.wait_ge(dma_sem[k], 16 * cnt)

        with nc.Block() as block:
            @block.sync
            def _(eng):
                run_engine("sync", eng, True)

            @block.scalar
            def _(eng):
                run_engine("scalar", eng, False)

            @block.vector
            def _(eng):
                run_engine("vector", eng, False)

            @block.gpsimd
            def _(eng):
                run_engine("gpsimd", eng, False)

            @block.tensor
            def _(eng):
                run_engine("tensor", eng, False)


def I(meth, *args, **kw):
    return (meth, args, kw)


class Rot:
    def __init__(self, items):
        self.items = list(items)
        self.i = 0

    def next(self):
        it = self.items[self.i % len(self.items)]
        self.i += 1
        return it


def blk_tiles(blk):
    t = [(i * 128, 128) for i in range(8)]
    if blk == 1:
        t.append((1024, 64))
    return t


def blk_supers(blk):
    s = [(0, 512), (512, 512)]
    if blk == 1:
        s.append((1024, 64))
    return s


def tks(lo, n):
    return list(range(lo // 128, (lo + n - 1) // 128 + 1))


def build_nc(debug=(), stop_after=None, hstop=None):
    nc = bass.Bass("TRN2", target_bir_lowering=False)

    def din(name, shape):
        return nc.dram_tensor(name, list(shape), F32, kind="ExternalInput").ap()

    def dout(name, shape):
        return nc.dram_tensor(name, list(shape), F32, kind="ExternalOutput").ap()

    xT_p = din("xT_p", [128, 8, SEQ])
    x_p = din("x_p", [SEQ, D])
    xT_s = din("xT_s", [128, 8, 64])
    x_s = din("x_s", [64, D])
    sconvT = din("sconvT", [128, 8, NSEQ, 2])
    sC = din("sC", [NSEQ, NH, DH, DH])
    snT = din("snT", [128, 2, NSEQ, NH])
    smT = din("smT", [NH, NSEQ])
    w_in_l = din("w_in_l", [36, 128, 8, 256])
    w_g = din("w_g", [128, 8, 8])
    bgate = din("bgate", [4, 2])
    convw = din("convw", [128, 8, 3])
    wco_l = din("wco_l", [4, 128, 8, 256])
    mhg_l = din("mhg_l", [128, 8])
    wmo_l = din("wmo_l", [4, 128, 8, 256])
    wo_l = din("wo_l", [4, 128, 8, 256])
    ln1g = din("ln1g", [1, D])
    ln1b = din("ln1b", [1, D])
    wff1_l = din("wff1_l", [16, 128, 8, 256])
    wff2_l = din("wff2_l", [8, 2, 128, 4, 512])
    ln2g = din("ln2g", [1, D])
    ln2b = din("ln2b", [1, D])
    c_ident = din("c_ident", [128, 128])
    c_masks = din("c_masks", [128, 192])
    c_sel4 = din("c_sel4", [4, 4, 128])
    c_selm = din("c_selm", [4, 4, 16])
    c_bmask = din("c_bmask", [128, 16, 64])
    c_rmask = din("c_rmask", [64, 16])
    c_r01 = din("c_r01", [4, 2, 64])
    y_p = dout("y_p", [SEQ, D])
    y_s = dout("y_s", [64, D])
    conv_p = dout("conv_p", [128, 8, 2])
    conv_s = dout("conv_s", [128, 8, NSEQ, 2])
    C_p = dout("C_p", [NH, DH, DH])
    C_s = dout("C_s", [NSEQ, NH, DH, DH])
    n_p = dout("n_p", [128, NH, 2])
    n_s = dout("n_s", [128, NSEQ, NH, 2])
    m_p = dout("m_p", [NH, 1])
    m_s = dout("m_s", [NH, NSEQ])

    P = Prog(nc)
    dbg_outs = {}

    with ExitStack() as top:
        uid = [0]

        def sb(es, name, shape, dt=F32):
            uid[0] += 1
            return es.enter_context(nc.sbuf_tensor("%s_u%d" % (name, uid[0]), list(shape), dt))

        def V(fn, r=(), w=()):
            return P.op("vector", fn, r, w)

        def A(fn, r=(), w=()):
            return P.op("scalar", fn, r, w)

        def G(fn, r=(), w=()):
            return P.op("gpsimd", fn, r, w)

        def T(fn, r=(), w=()):
            return P.op("tensor", fn, r, w)

        def LD(out, in_, w, semkey, r=()):
            return P.dma("sync", I("dma_start", out=out, in_=in_), reads=r, writes=w, semkey=semkey)

        def ST(out, in_, r, semkey, w=(), eng=None):
            return P.dma(eng or ST_ENG, I("dma_start", out=out, in_=in_), reads=r, writes=w, semkey=semkey)

        def dbg(name, ap, keys, shape):
            if name in debug:
                d = nc.dram_tensor("dbg_" + name, list(shape), ap.dtype, kind="ExternalOutput").ap()
                dbg_outs[name] = d
                ST(d, ap, keys, ("dbg", name))

        pbs = [top.enter_context(nc.psum_tensor("pb%d" % i, [128, 512], F32)) for i in range(8)]
        PB = [("pb", i) for i in range(8)]
        pT = pbs[7][:].bitcast(BF16)
        KPT = PB[7]

        identf = sb(top, "identf", [128, 128])
        identb = sb(top, "identb", [128, 128], BF16)
        masks = sb(top, "masks", [128, 192])
        sel4 = sb(top, "sel4", [4, 4, 128])
        selm = sb(top, "selm", [4, 4, 16])
        bmask = sb(top, "bmask", [128, 16, 64], BF16)
        bmaskf = None
        rmask = sb(top, "rmask", [64, 16])
        r01 = sb(top, "r01", [4, 2, 64])
        ones4 = sb(top, "ones4", [4, 128])
        cw = sb(top, "cw", [128, 8, 3])
        mhg = sb(top, "mhg", [128, 8])
        bg_t = sb(top, "bg_t", [4, 2])
        nbf = sb(top, "nbf", [4, 1])
        m0T = sb(top, "m0T", [4, NSEQ])
        snT_t = sb(top, "snT_t", [128, 2, NSEQ, NH])
        scv = sb(top, "scv", [128, 8, NSEQ, 2])
        Cst = sb(top, "Cst", [128, NH, 2, 257])
        Cbf = sb(top, "Cbf", [128, NH, 2, 257], BF16)
        carry = sb(top, "carry", [4, 2])
        convcar = sb(top, "convcar", [128, 8, 2])
        convout = sb(top, "convout", [128, 8, 2])
        convsout = sb(top, "convsout", [128, 8, NSEQ, 2])
        nsout = sb(top, "nsout", [128, NSEQ, NH, 2])
        npout = sb(top, "npout", [128, NH, 2])
        msout = sb(top, "msout", [4, 17])
        NWS = 3
        NWB = 8
        wst = [sb(top, "wst%d" % i, [128, 8, 256]) for i in range(NWS)]
        wbt = [sb(top, "wbt%d" % i, [128, 8, 256], BF16) for i in range(NWB)]
        wst_rot = Rot(range(NWS))
        wb_free = list(range(NWB))

        LD(identf[:], c_ident, ["identf"], "identf")
        LD(masks[:], c_masks, ["masks"], "masks")
        LD(sel4[:], c_sel4, ["sel4"], "sel4")
        LD(selm[:], c_selm, ["selm"], "selm")
        LD(rmask[:], c_rmask, ["rmask"], "rmask")
        LD(r01[:], c_r01, ["r01"], "r01")
        LD(cw[:], convw, ["cw"], "cw")
        LD(mhg[:], mhg_l, ["mhg"], "mhg")
        LD(bg_t[:], bgate, ["bg_t"], "bg_t")
        LD(m0T[:], smT, ["m0T"], "m0T")
        LD(snT_t[:], snT, ["snT"], "snT")
        LD(scv[:], sconvT, ["scv"], "scv")
        with ExitStack() as es0:
            bmf = sb(es0, "bmf", [128, 16, 64])
            LD(bmf[:], c_bmask, ["bmf"], "bmf")
            V(I("tensor_copy", out=bmask[:], in_=bmf[:]), ["bmf"], ["bmask"])
            V(I("tensor_copy", out=identb[:], in_=identf[:]), ["identf"], ["identb"])
            V(I("tensor_scalar", out=nbf[:], in0=bg_t[:, 1:2], scalar1=-1.0, scalar2=None, op0=ALU.mult), ["bg_t"], ["nbf"])
            V(I("memset", ones4[:], 1.0), [], ["ones4"])
            V(I("memset", Cst[:], 0.0), [], ["Cst"])
            V(I("memset", Cbf[:], 0.0), [], ["Cbf"])
            V(I("memset", carry[:], 0.0), [], ["carry"])
            V(I("memset", convcar[:], 0.0), [], ["convcar"])
            P.barrier()

        def w_load(dram_blk, scale_mhg=False, view4=False):
            si = wst_rot.next()
            assert wb_free, "weight pool exhausted"
            bi = wb_free.pop(0)
            stg, wb = wst[si], wbt[bi]
            if view4:
                dst = stg[:].rearrange("p a b -> p (a b)").rearrange("p (k c) -> p k c", k=4)
            else:
                dst = stg[:]
            LD(dst, dram_blk, [("wst", si)], ("wst", si))
            ce = CAST_PATTERN[cast_rr[0] % len(CAST_PATTERN)]
            cast_rr[0] += 1
            if scale_mhg:
                if ce == "gpsimd":
                    ce = "vector"
                if ce == "scalar":
                    P.op("scalar", [I("activation", out=wb[:, k, :], in_=stg[:, k, :], func=AF.Copy, scale=mhg[:, k:k + 1])
                                    for k in range(8)], [("wst", si), "mhg"], [("wb", bi)])
                else:
                    P.op("vector", [I("tensor_scalar", out=wb[:, k, :], in0=stg[:, k, :], scalar1=mhg[:, k:k + 1],
                                      scalar2=None, op0=ALU.mult) for k in range(8)],
                         [("wst", si), "mhg"], [("wb", bi)])
            elif ce == "scalar":
                A(I("activation", out=wb[:], in_=stg[:], func=AF.Copy), [("wst", si)], [("wb", bi)])
            elif ce == "vector":
                V(I("tensor_copy", out=wb[:], in_=stg[:]), [("wst", si)], [("wb", bi)])
            else:
                G(I("tensor_copy", out=wb[:], in_=stg[:]), [("wst", si)], [("wb", bi)])
            if view4:
                ap = wb[:].rearrange("p a b -> p (a b)").rearrange("p (k c) -> p k c", k=4)
            else:
                ap = wb[:]
            return ap, ("wb", bi), bi

        pre = {}

        def w_get(key, dram_blk, **kw):
            if key in pre:
                return pre.pop(key)
            return w_load(dram_blk, **kw)

        def w_prefetch(items):
            for key, dram_blk, kw in items:
                pre[key] = w_load(dram_blk, **kw)

        def head_items(h):
            return [(("w_in", 12 + h), w_in_l[12 + h], {}), (("w_in", 16 + h), w_in_l[16 + h], {}),
                    (("w_in", 20 + h), w_in_l[20 + h], {}), (("w_in", 24 + h), w_in_l[24 + h], {})]

        def conv_items(j):
            return [(("w_in", j), w_in_l[j], {}), (("w_in", 4 + j), w_in_l[4 + j], {}), (("w_in", 8 + j), w_in_l[8 + j], {})]

        def merge_items(j):
            return [(("wco", j), wco_l[j], {}), (("wmo", j), wmo_l[j], dict(scale_mhg=True)),
                    (("w_in", 28 + j), w_in_l[28 + j], {}), (("w_in", 32 + j), w_in_l[32 + j], {})]

        def wo_items():
            return [(("wo", i), wo_l[i], {}) for i in range(4)]

        def ffn_items(g):
            return [(("ff1", 2 * g), wff1_l[2 * g], {}), (("ff1", 2 * g + 1), wff1_l[2 * g + 1], {}),
                    (("ff2", g, 0), wff2_l[g, 0], dict(view4=True)), (("ff2", g, 1), wff2_l[g, 1], dict(view4=True))]

        def get_items(items):
            return tuple(w_get(k, b_, **kw) for k, b_, kw in items)

        def w_free(*slots):
            for s in slots:
                wb_free.append(s)

        stage_count = [0]
        cast_rr = [0]

        def stop_here(tag):
            return stop_after is not None and tag == stop_after

        stopped = False
        for blk in range(2):
            if stopped:
                break
            tiles = blk_tiles(blk)
            supers = blk_supers(blk)
            NT = 1024 if blk == 0 else 1088
            NTI = len(tiles)
            with ExitStack() as esM:
                mergedT_raw = sb(esM, "mergedT%d" % blk, [128, 4 * NT], F32)
                mergedT = mergedT_raw[:].bitcast(BF16).rearrange("p (a b) -> p a b", a=8)
                with ExitStack() as esA:
                    xT = sb(esA, "xT%d" % blk, [128, 8, NT], BF16)
                    ucT_raw = sb(esA, "ucT%d" % blk, [128, 4 * NT], F32)
                    ucT = ucT_raw[:].bitcast(BF16).rearrange("p (a b) -> p a b", a=8)
                    hnT = sb(esA, "hnT%d" % blk, [128, 8, NT], BF16)
                    a_row = sb(esA, "a_row%d" % blk, [4, NT])
                    tokm = sb(esA, "tokm%d" % blk, [128, NTI, 16])
                    decb = sb(esA, "decb%d" % blk, [128, 4, 24])

                    P.stage = 'b%d_s0_x' % blk
                    for pi, lo in enumerate(range(0, 1024, 256)):
                        si = wst_rot.next()
                        LD(wst[si][:], xT_p[:, :, blk * 1024 + lo: blk * 1024 + lo + 256], [("wst", si)], ("wst", si))
                        A(I("activation", out=xT[:, :, lo:lo + 256], in_=wst[si][:], func=AF.Copy),
                          [("wst", si)], [("xT", t) for t in tks(lo, 256)])
                    if blk == 1:
                        si = wst_rot.next()
                        LD(wst[si][:, :, 0:64], xT_s, [("wst", si)], ("wst", si))
                        A(I("activation", out=xT[:, :, 1024:1088], in_=wst[si][:, :, 0:64], func=AF.Copy),
                          [("wst", si)], [("xT", 8)])
                    dbg("xT%d" % blk, xT[:], [("xT", t) for t in range(NTI)], [128, 8, NT])

                    P.stage = 'b%d_s1a_gates' % blk
                    with ExitStack() as esG:
                        GK = ["gates"]
                        wgs = sb(esA, "wgs", [128, 8, 8])
                        wgb = sb(esA, "wgb", [128, 8, 8], BF16)
                        li = ucT_raw[0:4, 0:NT]
                        lf = ucT_raw[0:4, NT:2 * NT]
                        Bc = ucT_raw[0:4, 2 * NT:3 * NT]
                        Mr = ucT_raw[0:4, 3 * NT:4 * NT]
                        tmp = mergedT_raw[0:4, 0:NT]
                        zer = mergedT_raw[0:4, NT:NT + 1024]
                        mprev = sb(esA, "mprev", [4, 24])
                        mend = sb(esA, "mend", [4, 24])
                        dec = sb(esA, "dec", [4, 24])
                        decx = sb(esA, "decx", [4, 4, 24])
                        LD(wgs[:], w_g, ["wgs"], "wgs")
                        V(I("tensor_copy", out=wgb[:], in_=wgs[:]), ["wgs"], ["wgb"])
                        V(I("memset", zer[:], 0.0), [], ["zer"])
                        V(I("memset", decx[:], 0.0), [], GK)
                        V(I("memset", dec[:], 0.0), [], GK)
                        V(I("memset", mprev[:], 0.0), [], GK)
                        V(I("memset", mend[:], 0.0), [], GK)
                        for si_, (lo, n) in enumerate(supers):
                            b0, b1 = (0, 1) if si_ % 2 == 0 else (2, 3)
                            T([I("matmul", pbs[b0][0:4, 0:n], lhsT=wgb[:, k, 0:4], rhs=xT[:, k, lo:lo + n],
                                                                          start=(k == 0), stop=(k == 7)) for k in range(8)],
                              ["wgb"] + [("xT", t) for t in tks(lo, n)], [PB[b0]])
                            T([I("matmul", pbs[b1][0:4, 0:n], lhsT=wgb[:, k, 4:8], rhs=xT[:, k, lo:lo + n],
                                                                          start=(k == 0), stop=(k == 7)) for k in range(8)],
                              ["wgb"] + [("xT", t) for t in tks(lo, n)], [PB[b1]])
                            A(I("activation", out=li[:, lo:lo + n], in_=pbs[b0][0:4, 0:n], func=AF.Identity,
                                                                        bias=bg_t[:, 0:1], scale=1.0),
                              [PB[b0], "bg_t"], [("li", lo)])
                            A(I("activation", out=tmp[:, lo:lo + n], in_=pbs[b1][0:4, 0:n], func=AF.Exp,
                                                                        bias=nbf[:, 0:1], scale=-1.0),
                              [PB[b1], "nbf"], [("gtmp", lo)])
                            A(I("activation", out=lf[:, lo:lo + n], in_=tmp[:, lo:lo + n], func=AF.Ln, bias=1.0, scale=1.0),
                              [("gtmp", lo)], [("lf", lo)])
                        allg = [("li", lo) for lo, n in supers] + [("lf", lo) for lo, n in supers] + [("gtmp", lo) for lo, n in supers]
                        V(I("tensor_scalar", out=lf[:], in0=lf[:], scalar1=-1.0, scalar2=None, op0=ALU.mult), allg + ["zer"], GK)
                        init_b = 0.0 if blk == 0 else carry[:, 0:1]
                        init_m = 0.0 if blk == 0 else carry[:, 1:2]
                        V(I("tensor_tensor_scan", out=Bc[:, 0:1024], data0=zer[:], data1=lf[:, 0:1024], initial=init_b,
                                                         op0=ALU.add, op1=ALU.add), GK + ["carry"], GK)
                        V(I("tensor_tensor", out=a_row[:, 0:1024], in0=li[:, 0:1024], in1=Bc[:, 0:1024], op=ALU.subtract), GK, GK + ["a_row"])
                        V(I("tensor_tensor_scan", out=Mr[:, 0:1024], data0=zer[:], data1=a_row[:, 0:1024], initial=init_m,
                                                         op0=ALU.add, op1=ALU.max), GK + ["carry", "a_row"], GK)
                        if blk == 0:
                            V(I("memset", mprev[:, 0:1], 0.0), GK, GK)
                        else:
                            V(I("tensor_copy", out=mprev[:, 0:1], in_=carry[:, 1:2]), GK + ["carry"], GK)
                        V(I("tensor_copy", out=mprev[:, 1:8], in_=Mr[:, 127:1023:128][:, 0:7]), GK, GK)
                        V(I("tensor_copy", out=mend[:, 0:8], in_=Mr[:, 127:1024:128]), GK, GK)
                        if blk == 0:
                            V(I("tensor_copy", out=carry[:, 0:1], in_=Bc[:, 1023:1024]), GK, GK + ["carry"])
                            V(I("tensor_copy", out=carry[:, 1:2], in_=Mr[:, 1023:1024]), GK, GK + ["carry"])
                        if blk == 1:
                            s3 = lambda t: t[:, 1024:1088].rearrange("p (b t) -> p b t", t=4)
                            V(I("tensor_tensor_scan", out=Bc[:, 1024:1088], data0=r01[:, 0, :], data1=lf[:, 1024:1088], initial=0.0,
                                                             op0=ALU.mult, op1=ALU.add), GK + ["r01"], GK)
                            V(I("tensor_tensor", out=a_row[:, 1024:1088], in0=li[:, 1024:1088], in1=Bc[:, 1024:1088], op=ALU.subtract),
                              GK, GK + ["a_row"])
                            V(I("tensor_copy", out=tmp[:, 1024:1088], in_=a_row[:, 1024:1088]), GK + ["a_row"], GK)
                            V(I("tensor_tensor", out=s3(tmp)[:, :, 0:1], in0=s3(a_row)[:, :, 0:1], in1=m0T[:].unsqueeze(2), op=ALU.max),
                              GK + ["a_row", "m0T"], GK)
                            V(I("tensor_tensor_scan", out=Mr[:, 1024:1088], data0=r01[:, 1, :], data1=tmp[:, 1024:1088], initial=0.0,
                                                             op0=ALU.add, op1=ALU.max), GK + ["r01"], GK)
                            V(I("tensor_copy", out=mprev[:, 8:24], in_=m0T[:]), GK + ["m0T"], GK)
                            V(I("tensor_copy", out=mend[:, 8:24], in_=Mr[:, 1027:1088:4]), GK, GK)
                        ncol = 8 if blk == 0 else 24
                        V(I("tensor_tensor", out=dec[:, 0:ncol], in0=mprev[:, 0:ncol], in1=mend[:, 0:ncol], op=ALU.subtract), GK, GK)
                        A(I("activation", out=dec[:, 0:ncol], in_=dec[:, 0:ncol], func=AF.Exp), GK, GK)
                        V(I("tensor_tensor", out=decx[:, :, 0:ncol], in0=dec[:, 0:ncol].unsqueeze(1).to_broadcast([4, 4, ncol]),
                                                    in1=selm[:, :, 0:1].to_broadcast([4, 4, ncol]), op=ALU.mult), GK + ["selm"], GK)
                        T(I("matmul", pbs[4][:, 0:4 * 24], lhsT=ones4[:], rhs=decx[:].rearrange("p a b -> p (a b)"), start=True, stop=True),
                          GK + ["ones4"], [PB[4]])
                        V(I("tensor_copy", out=decb[:].rearrange("p a b -> p (a b)"), in_=pbs[4][:, 0:96]), [PB[4]], ["decb"])
                        def views(t, lo, n):
                            return t[:, lo:lo + n]
                        quant = []
                        def transpose_rows(src, qi):
                            T([I("transpose", out=pbs[5][0:n, ti * 16 + qi * 4: ti * 16 + (qi + 1) * 4], in_=src[:, lo:lo + n],
                                 identity=identf[0:4, 0:4]) for ti, (lo, n) in enumerate(tiles)],
                              GK + ["identf"], [PB[5]])
                        def seg3(t, blkpart):
                            if blkpart == "p":
                                return t[:, 0:1024].rearrange("p (c t) -> p c t", t=128)
                            return t[:, 1024:1088].rearrange("p (b t) -> p b t", t=4)
                        def bc3(t, blkpart):
                            if blkpart == "p":
                                return t[:, 0:8].unsqueeze(2).to_broadcast([4, 8, 128])
                            return t[:, 8:24].unsqueeze(2).to_broadcast([4, 16, 4])
                        parts = ["p"] + (["s"] if blk == 1 else [])
                        V(I("tensor_scalar", out=tmp[:], in0=Mr[:], scalar1=-1.0, scalar2=None, op0=ALU.mult), GK, GK)
                        transpose_rows(tmp, 0)
                        for bp in parts:
                            V(I("tensor_tensor", out=seg3(tmp, bp), in0=bc3(mprev, bp), in1=seg3(Mr, bp), op=ALU.subtract),
                              GK + [PB[5]] + [("tokm", t) for t in range(NTI)], GK)
                        A(I("activation", out=tmp[:], in_=tmp[:], func=AF.Exp), GK, GK)
                        transpose_rows(tmp, 1)
                        V(I("tensor_tensor", out=tmp[:], in0=Bc[:], in1=Mr[:], op=ALU.add), GK + [PB[5]] + [("tokm", t) for t in range(NTI)], GK)
                        if blk == 1:
                            V(I("tensor_copy", out=msout[:, 0:1], in_=tmp[:, 1023:1024]), GK, ["msout"])
                            V(I("tensor_copy", out=msout[:, 1:17], in_=tmp[:, 1027:1088:4]), GK, ["msout"])
                            ST(m_p, msout[:, 0:1], ["msout"], "m_p")
                            ST(m_s, msout[:, 1:17], ["msout"], "m_s")
                        A(I("activation", out=tmp[:], in_=tmp[:], func=AF.Exp, scale=-1.0), GK, GK)
                        transpose_rows(tmp, 2)
                        for bp in parts:
                            V(I("tensor_tensor", out=seg3(tmp, bp), in0=seg3(a_row, bp), in1=bc3(mend, bp), op=ALU.subtract),
                              GK + ["a_row", PB[5]] + [("tokm", t) for t in range(NTI)], GK)
                        A(I("activation", out=tmp[:], in_=tmp[:], func=AF.Exp), GK, GK)
                        transpose_rows(tmp, 3)
                        V(I("tensor_copy", out=tokm[:].rearrange("p a b -> p (a b)"), in_=pbs[5][:, 0:NTI * 16]),
                          [PB[5]], [("tokm", t) for t in range(NTI)] + GK)
                        dbg("tokm%d" % blk, tokm[:], [("tokm", t) for t in range(NTI)], [128, NTI, 16])
                        dbg("arow%d" % blk, a_row[:], ["a_row"], [4, NT])
                        dbg("decb%d" % blk, decb[:], ["decb"], [128, 4, 24])
                        if not GATES_OVERLAP:
                            P.barrier()
                    if stop_here("gates%d" % blk):
                        stopped = True
                        break

                    P.stage = 'b%d_s1b_heads' % blk
                    with ExitStack() as esH:
                        qT = sb(esH, "qT", [128, 2, NT], BF16)
                        kT = sb(esH, "kT", [128, 2, NT], BF16)
                        ktok = sb(esH, "ktok", [128, NTI, 256], BF16)
                        vtok = sb(esH, "vtok", [128, NTI, 258], BF16)
                        numb = sb(esH, "numb", [128, NTI, 257])
                        stats = sb(esH, "stats", [128, NTI, 6])
                        mv = sb(esH, "mv", [128, NTI, 2])
                        sc = sb(esH, "sc", [128, 8, NTI])
                        dmm = [sb(esH, "dmm%d" % i, [128, 128]) for i in range(NPM)]
                        Pm = [sb(esH, "Pm%d" % i, [128, 128]) for i in range(NPM)]
                        Sb = [sb(esH, "Sb%d" % i, [128, 128], BF16) for i in range(2)]
                        STb = [sb(esH, "STb%d" % i, [128, 128], BF16) for i in range(2)]
                        tmpn = [sb(esH, "tmpn%d" % i, [128, 257]) for i in range(2)]
                        ksc = [sb(esH, "ksc%d" % i, [128, 256], BF16) for i in range(2)]
                        osig = [sb(esH, "osig%d" % i, [128, 256]) for i in range(2)]
                        hn = [sb(esH, "hn%d" % i, [128, 256]) for i in range(2)]
                        hb = [sb(esH, "hb%d" % i, [128, 256], BF16) for i in range(2)]
                        if blk == 1:
                            NC0R = 3
                            C0f = [sb(esH, "C0f%d" % i, [128, 2, 257])[:] for i in range(NC0R)]
                            C0b = [sb(esH, "C0b%d" % i, [128, 2, 257], BF16)[:] for i in range(NC0R)]
                            for i in range(NC0X):
                                C0f.append(ucT_raw[:, i * 514:(i + 1) * 514].rearrange("p (s v) -> p s v", s=2))
                                C0b.append(mergedT_raw[:, i * 257:(i + 1) * 257].bitcast(BF16).rearrange("p (s v) -> p s v", s=2))
                            NC0 = NC0R + NC0X
                            qTm = sb(esH, "qTm", [128, 2, 8, 64], BF16)
                            kscm = sb(esH, "kscm", [64, 8, 256], BF16)
                        u_off = 3598 if blk == 1 else 0
                        m_off = 1800 if blk == 1 else 0
                        Cbf2 = ucT_raw[:, u_off:u_off + 257].bitcast(BF16).rearrange("p (s v) -> p s v", s=2)
                        osig = [mergedT_raw[:, m_off + i * 256:m_off + (i + 1) * 256] for i in range(8)] + [osig[0][:]]
                        V(I("memset", vtok[:], 1.0), [], ["vtok_init"])
                        V(I("memset", numb[:], 0.0), [], ["numb_init"])
                        V(I("memset", stats[:], 0.0), [], ["stats_init"])
                        V(I("memset", mv[:], 1.0), [], ["mv_init"])
                        if not GATES_OVERLAP:
                            P.barrier()
                        rr = [0]

                        def nxt(lst):
                            rr[0] += 1
                            return rr[0] % len(lst)

                        pending = None
                        if hstop == 0:
                            P.enabled = False
                        for h in range(NH):
                            if hstop == 0.5 and h == 0 and pending is not None:
                                P.enabled = False
                            wq, wk, wv, wo = get_items(head_items(h))
                            if h + 1 < NH:
                                w_prefetch(head_items(h + 1))
                            else:
                                w_prefetch(conv_items(0))
                            HK = ("h", h)
                            P.stage = 'b%d_s1b_h%d_inproj' % (blk, h)
                            if hstop == 0.5 and h == 0:
                                P.enabled = False
                            bi_ = 0
                            for (wt, dst, scl, nm) in ((wq, qT, 1.0, "qT"), (wk, kT, KSCALE, "kT")):
                                for sub in range(2):
                                    for (lo, n) in supers:
                                        b = bi_ % 2
                                        bi_ += 1
                                        T([I("matmul",
                                            pbs[b][:, 0:n], lhsT=wt[0][:, k, sub * 128:(sub + 1) * 128], rhs=xT[:, k, lo:lo + n],
                                            start=(k == 0), stop=(k == 7)) for k in range(8)],
                                          [wt[1]] + [("xT", t) for t in tks(lo, n)], [PB[b]])
                                        A(I("activation",
                                            out=dst[:, sub, lo:lo + n], in_=pbs[b][:, 0:n], func=AF.Copy, scale=scl),
                                          [PB[b]], [(nm, t) for t in tks(lo, n)])
                            if hstop == 1 and h == 0:
                                P.enabled = False
                            for ti, (lo, n) in enumerate(tiles):
                                b = 2 + ti % 2
                                if KTOK_TRANSPOSE:
                                    kview = pbs[b][:, 0:128].bitcast(BF16)
                                    T([I("matmul", pbs[b][0:n, 256:512], lhsT=xT[:, k, lo:lo + n], rhs=wv[0][:, k, :],
                                         start=(k == 0), stop=(k == 7)) for k in range(8)] +
                                      [I("transpose", out=kview[0:n, sub * 128:(sub + 1) * 128], in_=kT[:, sub, lo:lo + n], identity=identb[:])
                                       for sub in range(2)],
                                      [wv[1], ("xT", ti), ("kT", ti), "identb"], [PB[b]])
                                    A(I("activation", out=ktok[0:n, ti, :], in_=kview[0:n, :], func=AF.Copy),
                                      [PB[b]], [("ktok", ti)])
                                else:
                                    T([I("matmul", pbs[b][0:n, 0:256], lhsT=xT[:, k, lo:lo + n], rhs=wk[0][:, k, :],
                                         start=(k == 0), stop=(k == 7)) for k in range(8)] +
                                      [I("matmul", pbs[b][0:n, 256:512], lhsT=xT[:, k, lo:lo + n], rhs=wv[0][:, k, :],
                                         start=(k == 0), stop=(k == 7)) for k in range(8)],
                                      [wk[1], wv[1], ("xT", ti)], [PB[b]])
                                    A(I("activation", out=ktok[0:n, ti, :], in_=pbs[b][0:n, 0:256], func=AF.Copy, scale=KSCALE),
                                      [PB[b]], [("ktok", ti)])
                                (V(I("tensor_copy", out=vtok[0:n, ti, 0:256], in_=pbs[b][0:n, 256:512]),
                                   [PB[b], "vtok_init"], [("vtok", ti)]) if VTOK_ENG == "vector" else
                                 A(I("activation", out=vtok[0:n, ti, 0:256], in_=pbs[b][0:n, 256:512], func=AF.Copy),
                                   [PB[b], "vtok_init"], [("vtok", ti)]))
                            if hstop == 2 and h == 0:
                                P.enabled = False
                            P.stage = 'b%d_s1b_h%d_chunks' % (blk, h)
                            for ti, (lo, n) in enumerate(tiles):
                                if hstop == 4 and h == 0 and ti == 1:
                                    P.enabled = False
                                samp = (n == 64)
                                i2 = ti % 2
                                mk = masks[0:n, 128:192] if samp else masks[0:n, 0:128]
                                if MASK_MM:
                                    T([I("matmul", pbs[4][0:n, 0:n], lhsT=sel4[:, h, 0:n], rhs=a_row[:, lo:lo + n], start=True, stop=False),
                                       I("matmul", pbs[4][0:n, 0:n], lhsT=identf[0:n, 0:n], rhs=mk, start=False, stop=True)],
                                      ["sel4", "a_row", "identf", "masks"], [PB[4]])
                                    A(I("activation", out=Pm[ti % NPM][0:n, 0:n], in_=pbs[4][0:n, 0:n], func=AF.Exp,
                                        bias=tokm[0:n, ti, h:h + 1], scale=1.0),
                                      [PB[4], ("tokm", ti)], [("Pm", ti % NPM)])
                                else:
                                    T(I("matmul", pbs[4][0:n, 0:n], lhsT=sel4[:, h, 0:n], rhs=a_row[:, lo:lo + n], start=True, stop=True),
                                      ["sel4", "a_row"], [PB[4]])
                                    V(I("tensor_tensor", out=dmm[ti % NPM][0:n, 0:n], in0=pbs[4][0:n, 0:n], in1=mk, op=ALU.add),
                                      [PB[4], "masks"], [("dmm", ti % NPM)])
                                    A(I("activation", out=Pm[ti % NPM][0:n, 0:n], in_=dmm[ti % NPM][0:n, 0:n], func=AF.Exp,
                                        bias=tokm[0:n, ti, h:h + 1], scale=1.0),
                                      [("dmm", ti % NPM), ("tokm", ti)], [("Pm", ti % NPM)])
                                T([I("matmul", pbs[5][0:n, 0:n], lhsT=qT[:, sub, lo:lo + n], rhs=kT[:, sub, lo:lo + n],
                                                                           start=(sub == 0), stop=(sub == 1)) for sub in range(2)],
                                  [("qT", ti), ("kT", ti)], [PB[5]])
                                V(I("tensor_tensor", out=Sb[i2][0:n, 0:n], in0=pbs[5][0:n, 0:n], in1=Pm[ti % NPM][0:n, 0:n], op=ALU.mult),
                                  [PB[5], ("Pm", ti % NPM)], [("Sb", i2)])
                                T(I("transpose", out=pT[0:n, 0:n], in_=Sb[i2][0:n, 0:n], identity=identb[0:n, 0:n]),
                                  [("Sb", i2), "identb"], [KPT])
                                A(I("activation", out=STb[i2][0:n, 0:n], in_=pT[0:n, 0:n], func=AF.Copy),
                                  [KPT], [("STb", i2)])
                                T(I("matmul", pbs[6][0:n, 0:257], lhsT=STb[i2][0:n, 0:n], rhs=vtok[0:n, ti, 0:257],
                                                                        start=True, stop=True),
                                  [("STb", i2), ("vtok", ti), "vtok_init"], [PB[6]])
                                if hstop == 3 and h == 0 and ti == 0:
                                    P.enabled = False
                                G(I("tensor_scalar", out=ksc[i2][0:n, :], in0=ktok[0:n, ti, :],
                                                                               scalar1=tokm[0:n, ti, 12 + h:13 + h], scalar2=1.0, op0=ALU.mult, op1=ALU.mult),
                                  [("ktok", ti), ("tokm", ti)], [("ksc", i2)])
                                if not samp:
                                    c = ti
                                    T([I("matmul", pbs[0][0:n, 0:257], lhsT=qT[:, sub, lo:lo + n], rhs=(Cbf[:, h] if c % 2 == 0 else Cbf2)[:, sub, :],
                                                                               start=(sub == 0), stop=(sub == 1)) for sub in range(2)],
                                      [("qT", ti), (("Cbf", h) if c % 2 == 0 else "Cbf2")], [PB[0]])
                                    for sub in range(2):
                                        T(I("matmul", pbs[1 + sub][:, 0:257], lhsT=ksc[i2][0:n, sub * 128:(sub + 1) * 128],
                                                                                         rhs=vtok[0:n, ti, 0:257], start=True, stop=True),
                                          [("ksc", i2), ("vtok", ti), "vtok_init"], [PB[1 + sub]])
                                        V(I("scalar_tensor_tensor", out=Cst[:, h, sub, :], in0=Cst[:, h, sub, :],
                                                                                         scalar=decb[:, h, c:c + 1], in1=pbs[1 + sub][:, 0:257],
                                                                                         op0=ALU.mult, op1=ALU.add),
                                          [PB[1 + sub], "decb", ("Cst", h), "Cst"], [("Cst", h)])
                                    (V(I("tensor_copy", out=(Cbf[:, h] if (c + 1) % 2 == 0 else Cbf2), in_=Cst[:, h, :, :]), [("Cst", h), "gates"], [(("Cbf", h) if (c + 1) % 2 == 0 else "Cbf2")]) if CBF_ENG == "vector" else
                                     A(I("activation", out=(Cbf[:, h] if (c + 1) % 2 == 0 else Cbf2), in_=Cst[:, h, :, :], func=AF.Copy), [("Cst", h), "gates"], [(("Cbf", h) if (c + 1) % 2 == 0 else "Cbf2")]))
                                else:
                                    def fill_masked(hf):
                                        for sub in range(2):
                                            V(I("tensor_tensor", out=qTm[:, sub, :, :],
                                                in0=qT[:, sub, 1024:1088].unsqueeze(1).to_broadcast([128, 8, 64]),
                                                in1=bmask[:, hf * 8:(hf + 1) * 8, :], op=ALU.mult),
                                              [("qT", ti), "bmask"], ["qTm"])
                                        V(I("tensor_tensor", out=kscm[:], in0=ksc[i2][0:64, :].unsqueeze(1).to_broadcast([64, 8, 256]),
                                            in1=rmask[:, hf * 8:(hf + 1) * 8].unsqueeze(2).to_broadcast([64, 8, 256]), op=ALU.mult),
                                          [("ksc", i2), "rmask"], ["kscm"])
                                    for bq in range(NSEQ):
                                        if bq % 8 == 0:
                                            fill_masked(bq // 8)
                                        cs = (h * NSEQ + bq) % NC0
                                        LD(C0f[cs][:, :, 0:256], sC[bq, h].rearrange("(s p) v -> p s v", p=128), [("C0f", cs)], ("C0f", cs), r=["gates"])
                                        G(I("tensor_copy", out=C0f[cs][:, :, 256:257], in_=snT_t[:, :, bq, h:h + 1]),
                                          ["snT", "gates"], [("C0n", cs)], )
                                        A(I("activation", out=C0b[cs][:], in_=C0f[cs][:], func=AF.Copy), [("C0f", cs), ("C0n", cs), "gates"], [("C0b", cs)])
                                        T([I("matmul", pbs[0][0:64, 0:257], lhsT=qTm[:, sub, bq % 8, :], rhs=C0b[cs][:, sub, :],
                                                                                     start=(bq == 0 and sub == 0), stop=(bq == NSEQ - 1 and sub == 1))
                                           for sub in range(2)],
                                          ["qTm", ("C0b", cs)], [PB[0]])
                                        for sub in range(2):
                                            T(I("matmul", pbs[1 + sub][:, 0:257], lhsT=kscm[:, bq % 8, sub * 128:(sub + 1) * 128],
                                                                                        rhs=vtok[0:64, ti, 0:257], start=True, stop=True),
                                              ["kscm", ("vtok", ti), "vtok_init"], [PB[1 + sub]])
                                            V(I("scalar_tensor_tensor", out=C0f[cs][:, sub, :], in0=C0f[cs][:, sub, :],
                                                                                                      scalar=decb[:, h, 8 + bq:9 + bq], in1=pbs[1 + sub][:, 0:257],
                                                                                                      op0=ALU.mult, op1=ALU.add),
                                              [PB[1 + sub], "decb", ("C0f", cs), ("C0n", cs), ("C0b", cs)], [("C0f", cs), ("C0n", cs)])
                                        ST(C_s[bq, h].rearrange("(s p) v -> p s v", p=128), C0f[cs][:, :, 0:256], [("C0f", cs)], ("C0f", cs), eng=CS_ST_ENG)
                                        G(I("tensor_copy", out=nsout[:, bq, h, :], in_=C0f[cs][:, :, 256]),
                                          [("C0n", cs), ("C0f", cs)], ["nsout"])
                                A(I("activation", out=tmpn[i2][0:n, :], in_=pbs[0][0:n, 0:257], func=AF.Identity,
                                                                            scale=tokm[0:n, ti, 4 + h:5 + h], bias=0.0),
                                  [PB[0], ("tokm", ti)], [("tmpn", i2)])
                                V(I("tensor_tensor", out=numb[0:n, ti, :], in0=tmpn[i2][0:n, :], in1=pbs[6][0:n, 0:257], op=ALU.add),
                                  [("tmpn", i2), PB[6], "numb_init"], [("numb", ti)])
                                V(I("bn_stats", out=stats[0:n, ti, :], in_=numb[0:n, ti, 0:256]),
                                  [("numb", ti), "stats_init"], [("stats", ti)])
                            if hstop == 5 and h == 0:
                                P.enabled = False
                            P.stage = 'b%d_s1b_h%d_phaseB' % (blk, h)
                            allnum = [("numb", t) for t in range(NTI)] + [("stats", t) for t in range(NTI)] + [("tokm", t) for t in range(NTI)]
                            V([I("bn_aggr", out=mv[0:n, ti, :], in_=stats[0:n, ti, :]) for ti, (lo, n) in enumerate(tiles)], allnum + ["mv_init"], ["mv"])
                            regions = [(slice(0, 128), slice(0, 8))] + ([(slice(0, 64), slice(8, 9))] if blk == 1 else [])
                            for ri, (rw, ts_) in enumerate(regions):
                                den = numb[rw, ts_, 256]
                                SK = lambda i: ("sc", i, ri)
                                V(I("scalar_tensor_tensor", out=sc[rw, 0, ts_], in0=den, scalar=-1.0, in1=den, op0=ALU.mult, op1=ALU.max), allnum, [SK(0)])
                                V(I("tensor_tensor", out=sc[rw, 1, ts_], in0=sc[rw, 0, ts_], in1=tokm[rw, ts_, 8 + h], op=ALU.max), [SK(0)] + allnum, [SK(1)])
                                V(I("reciprocal", out=sc[rw, 2, ts_], in_=sc[rw, 1, ts_]), [SK(1)], [SK(2)])
                                V(I("tensor_tensor", out=sc[rw, 3, ts_], in0=sc[rw, 2, ts_], in1=sc[rw, 2, ts_], op=ALU.mult), [SK(2)], [SK(3)])
                                V(I("tensor_tensor", out=sc[rw, 4, ts_], in0=sc[rw, 3, ts_], in1=mv[rw, ts_, 1], op=ALU.mult), [SK(3), "mv"], [SK(4)])
                                V(I("tensor_scalar", out=sc[rw, 4, ts_], in0=sc[rw, 4, ts_], scalar1=LN_EPS, scalar2=None, op0=ALU.add), [SK(4)], [SK(4)])
                                A(I("activation", out=sc[rw, 5, ts_], in_=sc[rw, 4, ts_], func=AF.Sqrt), [SK(4)], [SK(5)])
                                V(I("reciprocal", out=sc[rw, 6, ts_], in_=sc[rw, 5, ts_]), [SK(5)], [SK(6)])
                                V(I("tensor_tensor", out=sc[rw, 7, ts_], in0=sc[rw, 6, ts_], in1=sc[rw, 2, ts_], op=ALU.mult), [SK(6), SK(2)], ["sc7"])
                            for ti, (lo, n) in enumerate(tiles):
                                i2 = ti % 2
                                T([I("matmul", pbs[3][0:n, 0:256], lhsT=xT[:, k, lo:lo + n], rhs=wo[0][:, k, :],
                                                                       start=(k == 0), stop=(k == 7)) for k in range(8)],
                                  [wo[1], ("xT", ti)], [PB[3]])
                                A(I("activation", out=osig[ti][0:n, :], in_=pbs[3][0:n, 0:256], func=AF.Sigmoid),
                                  [PB[3], "gates"], [("osig", ti)])
                                if HB_MODE == "fuse_act":
                                    V(I("scalar_tensor_tensor", out=hn[i2][0:n, :], in0=numb[0:n, ti, 0:256], scalar=mv[0:n, ti, 0:1],
                                        in1=osig[ti][0:n, :], op0=ALU.subtract, op1=ALU.mult),
                                      [("numb", ti), "mv", ("osig", ti)], [("hn", i2)])
                                    A(I("activation", out=hb[i2][0:n, :], in_=hn[i2][0:n, :], func=AF.Copy, scale=sc[0:n, 7, ti:ti + 1]),
                                      [("hn", i2), "sc7"], [("hb", i2)])
                                else:
                                    V(I("tensor_scalar", out=hn[i2][0:n, :], in0=numb[0:n, ti, 0:256], scalar1=mv[0:n, ti, 0:1],
                                        scalar2=sc[0:n, 7, ti:ti + 1], op0=ALU.subtract, op1=ALU.mult),
                                      [("numb", ti), "mv", "sc7"], [("hn", i2)])
                                    P.op("gpsimd" if HB_MODE == "pool" else "vector",
                                         I("tensor_tensor", out=hb[i2][0:n, :], in0=hn[i2][0:n, :], in1=osig[ti][0:n, :], op=ALU.mult),
                                         [("hn", i2), ("osig", ti)], [("hb", i2)])
                                T([I("transpose", out=pT[:, sub * 128: sub * 128 + n], in_=hb[i2][0:n, sub * 128:(sub + 1) * 128],
                                                                              identity=identb[0:n, 0:n]) for sub in range(2)],
                                  [("hb", i2), "identb"], [KPT])
                                A(I("activation", out=hnT[:, 2 * h:2 * h + 2, lo:lo + n],
                                                                     in_=pT[:, 0:256].rearrange("p (s t) -> p s t", t=128)[:, :, 0:n], func=AF.Copy),
                                  [KPT], [("hnT", h, ti)])
                            w_free(wq[2], wk[2], wv[2], wo[2])
                            if hstop == 6 and h == 0:
                                P.enabled = False
                        if blk == 1:
                            for h in range(NH):
                                ST(C_p[h].rearrange("(s p) v -> p s v", p=128), Cst[:, h, :, 0:256], [("Cst", h)], ("Cp", h))
                            G(I("tensor_copy", out=npout[:], in_=Cst[:, :, :, 256]), [("Cst", h) for h in range(NH)], ["npout"])
                            ST(n_p, npout[:], ["npout"], "npout")
                            ST(n_s, nsout[:], ["nsout"], "nsout")
                        dbg("hnT%d" % blk, hnT[:], [("hnT", h, t) for h in range(NH) for t in range(NTI)], [128, 8, NT])
                        dbg("qT%d" % blk, qT[:], [("qT", t) for t in range(NTI)], [128, 2, NT])
                        dbg("numb%d" % blk, numb[:], [("numb", t) for t in range(NTI)], [128, NTI, 257])
                        P.barrier()
                    if stop_here("heads%d" % blk):
                        stopped = True
                        break

                    P.stage = 'b%d_s1c_conv' % blk
                    with ExitStack() as esC:
                        ubuf = [sb(esC, "ubuf%d" % i, [128, 1026]) for i in range(2)]
                        usmp = [sb(esC, "usmp%d" % i, [128, NSEQ, 6]) for i in range(2)]
                        cgs = [sb(esC, "cgs%d" % i, [128, 512]) for i in range(2)]
                        ac0 = [sb(esC, "ac0%d" % i, [128, 512]) for i in range(2)]
                        ac1 = [sb(esC, "ac1%d" % i, [128, 512]) for i in range(2)]
                        P.barrier()
                        pend = None
                        it = 0
                        for j in range(4):
                            wbg, wcg, whc = get_items(conv_items(j))
                            if j < 3:
                                w_prefetch(conv_items(j + 1))
                            else:
                                w_prefetch(merge_items(0))
                            for sub in range(2):
                                cb = 2 * j + sub
                                ub = ubuf[cb % 2]
                                us = usmp[cb % 2]
                                UK = ("ubuf", cb % 2)
                                V(I("tensor_copy", out=ub[:, 0:2], in_=convcar[:, cb, :]), ["convcar", UK], [UK])
                                if blk == 1:
                                    V(I("tensor_copy", out=us[:, :, 0:2], in_=scv[:, cb, :, :]), ["scv", UK], [UK])
                                for (lo, n) in supers:
                                    samp = (n == 64)
                                    s_ = it % 2
                                    it += 1
                                    bb = (0, 1, 2) if s_ == 0 else (3, 4, 5)
                                    for wt, b in ((wcg, bb[0]), (whc, bb[1]), (wbg, bb[2])):
                                        T([I("matmul",
                                            pbs[b][:, 0:n], lhsT=wt[0][:, k, sub * 128:(sub + 1) * 128], rhs=xT[:, k, lo:lo + n],
                                            start=(k == 0), stop=(k == 7)) for k in range(8)],
                                          [wt[1]] + [("xT", t) for t in tks(lo, n)], [PB[b]])
                                    A(I("activation", out=cgs[s_][:, 0:n], in_=pbs[bb[0]][:, 0:n], func=AF.Copy),
                                      [PB[bb[0]]], [("cgs", s_)])
                                    if not samp:
                                        V(I("tensor_tensor", out=ub[:, 2 + lo:2 + lo + n], in0=pbs[bb[1]][:, 0:n],
                                                                                                     in1=cgs[s_][:, 0:n], op=ALU.mult),
                                          [PB[bb[1]], ("cgs", s_), UK], [UK])
                                        t0 = lambda o, ub=ub, lo=lo, n=n: ub[:, lo + o:lo + o + n]
                                        a0 = ac0[s_][:, 0:n]
                                        a1 = ac1[s_][:, 0:n]
                                        bgp = pbs[bb[2]][:, 0:n]
                                        dst = ucT[:, cb, lo:lo + n]
                                    else:
                                        V(I("tensor_tensor", out=us[:, :, 2:6], in0=pbs[bb[1]][:, 0:64].rearrange("p (b t) -> p b t", t=4),
                                                                                         in1=cgs[s_][:, 0:64].rearrange("p (b t) -> p b t", t=4), op=ALU.mult),
                                          [PB[bb[1]], ("cgs", s_), UK], [UK])
                                        t0 = lambda o, us=us: us[:, :, o:o + 4]
                                        a0 = ac0[s_][:, 0:64].rearrange("p (b t) -> p b t", t=4)
                                        a1 = ac1[s_][:, 0:64].rearrange("p (b t) -> p b t", t=4)
                                        bgp = pbs[bb[2]][:, 0:64].rearrange("p (b t) -> p b t", t=4)
                                        dst = ucT[:, cb, 1024:1088].rearrange("p (b t) -> p b t", t=4)
                                    G(I("tensor_scalar", out=a0, in0=t0(0), scalar1=cw[:, cb, 0:1], scalar2=1.0, op0=ALU.mult, op1=ALU.mult),
                                      [UK, "cw"], [("ac0", s_)])
                                    V(I("scalar_tensor_tensor", out=a1, in0=t0(1), scalar=cw[:, cb, 1:2], in1=a0,
                                                                                                  op0=ALU.mult, op1=ALU.add),
                                      [UK, "cw", ("ac0", s_)], [("ac1", s_)])
                                    V(I("scalar_tensor_tensor", out=a0, in0=t0(2), scalar=cw[:, cb, 2:3], in1=a1,
                                                                                                  op0=ALU.mult, op1=ALU.add),
                                      [UK, "cw", ("ac1", s_)], [("ac0", s_)])
                                    V(I("tensor_tensor", out=dst, in0=bgp, in1=a0, op=ALU.mult),
                                      [PB[bb[2]], ("ac0", s_)], [("ucT", cb, t) for t in tks(lo, n)])
                                if blk == 0:
                                    V(I("tensor_copy", out=convcar[:, cb, :], in_=ub[:, 1024:1026]), [UK, "convcar"], ["convcar"])
                                else:
                                    V(I("tensor_copy", out=convout[:, cb, :], in_=ub[:, 1024:1026]), [UK], ["convout"])
                                    V(I("tensor_copy", out=convsout[:, cb, :, :], in_=us[:, :, 4:6]), [UK], ["convsout"])
                            w_free(wbg[2], wcg[2], whc[2])
                        if blk == 1:
                            ST(conv_p, convout[:], ["convout"], "convout")
                            ST(conv_s, convsout[:], ["convsout"], "convsout")
                        dbg("ucT%d" % blk, ucT[:], [("ucT", cb, t) for cb in range(8) for t in range(NTI)], [128, 8, NT])
                        P.barrier()
                    if stop_here("conv%d" % blk):
                        stopped = True
                        break

                    P.stage = 'b%d_s2_merge' % blk
                    with ExitStack() as es2:
                        sg = [sb(es2, "sg%d" % i, [128, 512]) for i in range(4)]
                        t12 = [sb(es2, "t12%d" % i, [128, 512]) for i in range(4)]
                        P.barrier()
                        pend = None
                        it = 0
                        for j in range(4):
                            wc, wm, wgc, wgm = get_items(merge_items(j))
                            if j < 3:
                                w_prefetch(merge_items(j + 1))
                            else:
                                w_prefetch(wo_items())
                            for sub in range(2):
                                fb = 2 * j + sub
                                for (lo, n) in supers:
                                    s_ = it % 2
                                    it += 1
                                    bb = (0, 1, 2, 3) if s_ == 0 else (4, 5, 6, 7)
                                    srcs = ((wc, ucT, "ucT", bb[0]), (wm, hnT, "hnT", bb[1]), (wgc, xT, "xT", bb[2]), (wgm, xT, "xT", bb[3]))
                                    for wt, src, nm, b in srcs:
                                        if nm == "ucT":
                                            rk = [("ucT", cb, t) for cb in range(8) for t in tks(lo, n)]
                                        elif nm == "hnT":
                                            rk = [("hnT", hh, t) for hh in range(NH) for t in tks(lo, n)]
                                        else:
                                            rk = [("xT", t) for t in tks(lo, n)]
                                        T([I("matmul",
                                            pbs[b][:, 0:n], lhsT=wt[0][:, k, sub * 128:(sub + 1) * 128], rhs=src[:, k, lo:lo + n],
                                            start=(k == 0), stop=(k == 7)) for k in range(8)],
                                          [wt[1]] + rk, [PB[b]])
                                    A(I("activation", out=sg[2 * s_][:, 0:n], in_=pbs[bb[2]][:, 0:n], func=AF.Sigmoid),
                                      [PB[bb[2]]], [("sg", 2 * s_)])
                                    A(I("activation", out=sg[2 * s_ + 1][:, 0:n], in_=pbs[bb[3]][:, 0:n], func=AF.Sigmoid),
                                      [PB[bb[3]]], [("sg", 2 * s_ + 1)])
                                    V(I("tensor_tensor", out=t12[2 * s_][:, 0:n], in0=pbs[bb[0]][:, 0:n], in1=sg[2 * s_][:, 0:n], op=ALU.mult),
                                      [PB[bb[0]], ("sg", 2 * s_)], [("t12", 2 * s_)])
                                    V(I("tensor_tensor", out=t12[2 * s_ + 1][:, 0:n], in0=pbs[bb[1]][:, 0:n], in1=sg[2 * s_ + 1][:, 0:n], op=ALU.mult),
                                      [PB[bb[1]], ("sg", 2 * s_ + 1)], [("t12", 2 * s_ + 1)])
                                    G(I("tensor_tensor", out=mergedT[:, fb, lo:lo + n], in0=t12[2 * s_][:, 0:n],
                                                                                          in1=t12[2 * s_ + 1][:, 0:n], op=ALU.add),
                                      [("t12", 2 * s_), ("t12", 2 * s_ + 1)], [("mT", fb, t) for t in tks(lo, n)])
                            w_free(wc[2], wm[2], wgc[2], wgm[2])
                        dbg("mT%d" % blk, mergedT[:], [("mT", fb, t) for fb in range(8) for t in range(NTI)], [128, 8, NT])
                        P.barrier()
                if stopped or stop_here("merge%d" % blk):
                    stopped = True
                    break

                P.stage = 'b%d_s3_wo' % blk
                with ExitStack() as esB:
                    x1T = sb(esB, "x1T%d" % blk, [128, 8, NT], BF16)
                    acc = sb(esB, "acc%d" % blk, [128, NTI, D])
                    lng = sb(esB, "lng", [128, D])
                    lnb = sb(esB, "lnb", [128, D])
                    st2 = sb(esB, "st2", [128, 3, 2, 6])
                    mv2 = sb(esB, "mv2", [128, 3, 4])
                    with ExitStack() as es3:
                        xt = [sb(es3, "xt%d" % i, [128, D]) for i in range(NB3)]
                        s1 = [sb(es3, "s1%d" % i, [128, D]) for i in range(NB3)]
                        x1b = [sb(es3, "x1b%d" % i, [128, D], BF16) for i in range(NB3)]
                        P.barrier()
                        LD(lng[:], ln1g.partition_broadcast(128), ["lng"], "lng")
                        LD(lnb[:], ln1b.partition_broadcast(128), ["lnb"], "lnb")
                        wo4 = list(get_items(wo_items()))
                        w_prefetch(ffn_items(0))
                        for ti, (lo, n) in enumerate(tiles):
                            i2 = ti % NB3
                            bb = ((0, 1), (2, 3), (4, 5))[ti % 3]
                            src_x = x_p[blk * 1024 + lo: blk * 1024 + lo + n, :] if n == 128 else x_s
                            LD(xt[i2][0:n, :], src_x, [("xt", i2)], ("xt", i2))
                            for half in range(2):
                                for q4 in range(2):
                                    wt = wo4[2 * half + q4]
                                    T([I("matmul",
                                        pbs[bb[half]][0:n, q4 * 256:(q4 + 1) * 256], lhsT=mergedT[:, k, lo:lo + n], rhs=wt[0][:, k, :],
                                        start=(k == 0), stop=(k == 7)) for k in range(8)],
                                      [wt[1]] + [("mT", fb, ti) for fb in range(8)], [PB[bb[half]]])
                                V(I("scalar_tensor_tensor",
                                    out=s1[i2][0:n, half * 512:(half + 1) * 512], in0=xt[i2][0:n, half * 512:(half + 1) * 512], scalar=ALPHA,
                                    in1=pbs[bb[half]][0:n, :], op0=ALU.mult, op1=ALU.add),
                                  [("xt", i2), PB[bb[half]], ("s1", i2)], [("s1", i2)])
                                V(I("bn_stats", out=st2[0:n, i2, half, :], in_=s1[i2][0:n, half * 512:(half + 1) * 512]),
                                  [("s1", i2)], [("st2", i2, half)])
                            V(I("bn_aggr", out=mv2[0:n, i2, 0:2], in_=st2[0:n, i2, :, :].rearrange("p a b -> p (a b)")),
                              [("st2", i2, 0), ("st2", i2, 1)], [("mv2", i2)])
                            V(I("tensor_scalar", out=mv2[0:n, i2, 2:3], in0=mv2[0:n, i2, 1:2], scalar1=LN_EPS, scalar2=None, op0=ALU.add),
                              [("mv2", i2)], [("mv2", i2)])
                            A(I("activation", out=mv2[0:n, i2, 2:3], in_=mv2[0:n, i2, 2:3], func=AF.Sqrt), [("mv2", i2)], [("mv2", i2)])
                            V(I("reciprocal", out=mv2[0:n, i2, 3:4], in_=mv2[0:n, i2, 2:3]), [("mv2", i2)], [("mv2", i2)])
                            if LN_FUSE:
                                V(I("scalar_tensor_tensor", out=s1[i2][0:n, :], in0=s1[i2][0:n, :], scalar=mv2[0:n, i2, 0:1], in1=lng[0:n, :],
                                    op0=ALU.subtract, op1=ALU.mult), [("s1", i2), ("mv2", i2), "lng"], [("s1", i2)])
                                V(I("scalar_tensor_tensor", out=s1[i2][0:n, :], in0=s1[i2][0:n, :], scalar=mv2[0:n, i2, 3:4], in1=lnb[0:n, :],
                                    op0=ALU.mult, op1=ALU.add), [("s1", i2), ("mv2", i2), "lnb"], [("s1", i2)])
                            else:
                                V(I("tensor_scalar", out=s1[i2][0:n, :], in0=s1[i2][0:n, :], scalar1=mv2[0:n, i2, 0:1], scalar2=mv2[0:n, i2, 3:4],
                                                                        op0=ALU.subtract, op1=ALU.mult),
                                  [("s1", i2), ("mv2", i2)], [("s1", i2)])
                                V(I("tensor_tensor", out=s1[i2][0:n, :], in0=s1[i2][0:n, :], in1=lng[0:n, :], op=ALU.mult),
                                  [("s1", i2), "lng"], [("s1", i2)])
                                P.op(LN1B_ENG[ti % len(LN1B_ENG)], I("tensor_tensor", out=s1[i2][0:n, :], in0=s1[i2][0:n, :], in1=lnb[0:n, :], op=ALU.add),
                                     [("s1", i2), "lnb"], [("s1", i2)])
                            if ACC_ENG == "scalar":
                                A(I("activation", out=acc[0:n, ti, :], in_=s1[i2][0:n, :], func=AF.Copy, scale=ALPHA),
                                  [("s1", i2)], [("acc", ti)])
                            else:
                                G(I("tensor_scalar", out=acc[0:n, ti, :], in0=s1[i2][0:n, :], scalar1=ALPHA, scalar2=1.0, op0=ALU.mult, op1=ALU.mult),
                                  [("s1", i2)], [("acc", ti)])
                            A(I("activation", out=x1b[i2][0:n, :], in_=s1[i2][0:n, :], func=AF.Copy), [("s1", i2)], [("x1b", i2)])
                            T([I("transpose", out=pT[:, k * 128:k * 128 + n], in_=x1b[i2][0:n, k * 128:(k + 1) * 128],
                                                                      identity=identb[0:n, 0:n]) for k in range(8)],
                              [("x1b", i2), "identb"], [KPT])
                            if X1T_ENG == "vector":
                                V(I("tensor_copy", out=x1T[:, :, lo:lo + n], in_=pT[:].rearrange("p (k t) -> p k t", t=128)[:, :, 0:n]),
                                  [KPT], [("x1T", ti)])
                            else:
                                A(I("activation", out=x1T[:, :, lo:lo + n], in_=pT[:].rearrange("p (k t) -> p k t", t=128)[:, :, 0:n], func=AF.Copy),
                                  [KPT], [("x1T", ti)])
                        w_free(*[w[2] for w in wo4])
                        dbg("x1T%d" % blk, x1T[:], [("x1T", t) for t in range(NTI)], [128, 8, NT])
                        dbg("acc%d" % blk, acc[:], [("acc", t) for t in range(NTI)], [128, NTI, D])
                        P.barrier()
                    if stop_here("wo%d" % blk):
                        stopped = True
                        break
                    P.stage = 'b%d_s4_ffn' % blk
                    with ExitStack() as es4:
                        hidT = [sb(es4, "hidT%d" % i, [128, 4, FF_GRP], BF16) for i in range(NHID)]
                        rl = [sb(es4, "rl%d" % i, [128, FF_GRP]) for i in range(4)]
                        yt = [sb(es4, "yt%d" % i, [128, D]) for i in range(NYT)]
                        P.barrier()
                        LD(lng[:], ln2g.partition_broadcast(128), ["lng"], "lng")
                        LD(lnb[:], ln2b.partition_broadcast(128), ["lnb"], "lnb")
                        pairs = [(i * FF_GRP, FF_GRP) for i in range(1024 // FF_GRP)] + ([(1024, 64)] if blk == 1 else [])
                        pend = None
                        cnt = 0
                        for g in range(8):
                            w1a, w1b, w2a, w2b = get_items(ffn_items(g))
                            if g < 7:
                                w_prefetch(ffn_items(g + 1))
                            elif blk == 0:
                                w_prefetch(head_items(0))
                            for pi, (lo, n) in enumerate(pairs):
                                hd = hidT[(g * 5 + pi) % NHID]
                                HKY = ("hidT", (g * 5 + pi) % NHID)
                                for hs in range(4):
                                    wt = w1a if hs < 2 else w1b
                                    b = cnt % 4
                                    cnt += 1
                                    T([I("matmul",
                                        pbs[b][:, 0:n], lhsT=wt[0][:, k, (hs % 2) * 128:(hs % 2 + 1) * 128], rhs=x1T[:, k, lo:lo + n],
                                        start=(k == 0), stop=(k == 7)) for k in range(8)],
                                      [wt[1]] + [("x1T", t) for t in tks(lo, n)], [PB[b]])
                                    A(I("activation", out=rl[b][:, 0:n], in_=pbs[b][:, 0:n], func=AF.Relu), [PB[b]], [("rl", b)])
                                    P.op(SQ_ENG[cnt % len(SQ_ENG)], I("tensor_tensor", out=hd[:, hs, 0:n], in0=rl[b][:, 0:n], in1=rl[b][:, 0:n], op=ALU.mult),
                                         [("rl", b)], [HKY])
                                for tj, ti in enumerate(tks(lo, n)):
                                    tn = min(128, n)
                                    for half in range(2):
                                        wt = w2a if half == 0 else w2b
                                        b = 4 + (tj % 2) * 2 + half
                                        T([I("matmul",
                                            pbs[b][0:tn, :], lhsT=hd[:, hs, tj * 128:tj * 128 + tn], rhs=wt[0][:, hs, :],
                                            start=(hs == 0), stop=(hs == 3)) for hs in range(4)],
                                          [wt[1], HKY], [PB[b]])
                                        V(I("tensor_tensor", out=acc[0:tn, ti, half * 512:(half + 1) * 512],
                                                                                                  in0=acc[0:tn, ti, half * 512:(half + 1) * 512],
                                                                                                  in1=pbs[b][0:tn, :], op=ALU.add),
                                          [PB[b], ("acc", ti)], [("acc", ti)])
                            w_free(w1a[2], w1b[2], w2a[2], w2b[2])
                        for ti, (lo, n) in enumerate(tiles):
                            i2 = ti % NYT
                            for half in range(2):
                                V(I("bn_stats", out=st2[0:n, i2, half, :], in_=acc[0:n, ti, half * 512:(half + 1) * 512]),
                                  [("acc", ti)], [("st2", i2, half)])
                            V(I("bn_aggr", out=mv2[0:n, i2, 0:2], in_=st2[0:n, i2, :, :].rearrange("p a b -> p (a b)")),
                              [("st2", i2, 0), ("st2", i2, 1)], [("mv2", i2)])
                            V(I("tensor_scalar", out=mv2[0:n, i2, 2:3], in0=mv2[0:n, i2, 1:2], scalar1=LN_EPS, scalar2=None, op0=ALU.add),
                              [("mv2", i2)], [("mv2", i2)])
                            A(I("activation", out=mv2[0:n, i2, 2:3], in_=mv2[0:n, i2, 2:3], func=AF.Sqrt), [("mv2", i2)], [("mv2", i2)])
                            V(I("reciprocal", out=mv2[0:n, i2, 3:4], in_=mv2[0:n, i2, 2:3]), [("mv2", i2)], [("mv2", i2)])
                            if LN_FUSE:
                                V(I("scalar_tensor_tensor", out=yt[i2][0:n, :], in0=acc[0:n, ti, :], scalar=mv2[0:n, i2, 0:1], in1=lng[0:n, :],
                                    op0=ALU.subtract, op1=ALU.mult), [("acc", ti), ("mv2", i2), ("yt", i2), "lng"], [("yt", i2)])
                                V(I("scalar_tensor_tensor", out=yt[i2][0:n, :], in0=yt[i2][0:n, :], scalar=mv2[0:n, i2, 3:4], in1=lnb[0:n, :],
                                    op0=ALU.mult, op1=ALU.add), [("yt", i2), ("mv2", i2), "lnb"], [("yt", i2)])
                            else:
                                V(I("tensor_scalar", out=yt[i2][0:n, :], in0=acc[0:n, ti, :], scalar1=mv2[0:n, i2, 0:1],
                                                                               scalar2=mv2[0:n, i2, 3:4], op0=ALU.subtract, op1=ALU.mult),
                                  [("acc", ti), ("mv2", i2), ("yt", i2)], [("yt", i2)])
                                P.op(LN2G_ENG, I("tensor_tensor", out=yt[i2][0:n, :], in0=yt[i2][0:n, :], in1=lng[0:n, :], op=ALU.mult),
                                     [("yt", i2), "lng"], [("yt", i2)])
                                P.op(LN2B_ENG[ti % len(LN2B_ENG)], I("tensor_tensor", out=yt[i2][0:n, :], in0=yt[i2][0:n, :], in1=lnb[0:n, :], op=ALU.add),
                                     [("yt", i2), "lnb"], [("yt", i2)])
                            dsty = y_p[blk * 1024 + lo: blk * 1024 + lo + n, :] if n == 128 else y_s
                            ST(dsty, yt[i2][0:n, :], [("yt", i2)], ("yt", i2))
                        P.barrier()

        P.emit(lambda name: top.enter_context(nc.semaphore(name)))
    return nc, dbg_outs, P


def _consts():
    ident = np.eye(128, dtype=np.float32)
    masks = np.zeros((128, 192), np.float32)
    t = np.arange(128)[:, None]
    s = np.arange(128)[None, :]
    masks[:, 0:128] = np.where(s <= t, 0.0, NEG)
    t6 = np.arange(64)[:, None]
    s6 = np.arange(64)[None, :]
    bd = np.where((s6 // 4 == t6 // 4) & (s6 <= t6), 0.0, NEG)
    masks[0:64, 128:192] = bd
    sel4 = np.zeros((4, 4, 128), np.float32)
    selm = np.zeros((4, 4, 16), np.float32)
    for k in range(4):
        sel4[k, k, :] = 1.0
        selm[k, k, :] = 1.0
    bmask = np.zeros((128, 16, 64), np.float32)
    for b in range(16):
        bmask[:, b, 4 * b:4 * b + 4] = 1.0
    rmask = np.zeros((64, 16), np.float32)
    for p in range(64):
        rmask[p, p // 4] = 1.0
    r01 = np.zeros((4, 2, 64), np.float32)
    r01[:, 0, :] = 1.0
    r01[:, 0, 0::4] = 0.0
    r01[:, 1, 0::4] = -1.0e30
    return dict(c_ident=ident, c_masks=masks, c_sel4=sel4, c_selm=selm, c_bmask=bmask, c_rmask=rmask, c_r01=r01)


def _wblocks(w, nblk):
    return np.ascontiguousarray(w.reshape(8, 128, nblk, 256).transpose(2, 1, 0, 3))


def prep_inputs(x_prompt, x_sample, state_conv, state_C, state_n, state_m, w_in, b_gate, conv_w,
                w_conv_out, mh_g, w_m_out, w_o, ln1_g, ln1_b, w_ff1, w_ff2, ln2_g, ln2_b):
    f = lambda a: np.ascontiguousarray(np.asarray(a, dtype=np.float32))
    w_in0 = f(w_in)[0]
    w_main = np.concatenate([w_in0[:, :7168], w_in0[:, 7176:]], axis=1)
    shared = dict(
        w_in_l=_wblocks(w_main, 36),
        w_g=f(w_in0[:, 7168:7176].reshape(8, 128, 8).transpose(1, 0, 2)),
        bgate=f(f(b_gate)[0].reshape(2, 4).T),
        convw=f(f(conv_w)[0].reshape(3, 8, 128).transpose(2, 1, 0)),
        wco_l=_wblocks(f(w_conv_out)[0], 4),
        mhg_l=f(f(mh_g)[0].reshape(8, 128).T),
        wmo_l=_wblocks(f(w_m_out)[0], 4),
        wo_l=_wblocks(f(w_o)[0], 4),
        ln1g=f(ln1_g).reshape(1, D), ln1b=f(ln1_b).reshape(1, D),
        wff1_l=_wblocks(f(w_ff1)[0], 16),
        wff2_l=f(f(w_ff2)[0].reshape(8, 4, 128, 2, 512).transpose(0, 3, 2, 1, 4)),
        ln2g=f(ln2_g).reshape(1, D), ln2b=f(ln2_b).reshape(1, D),
    )
    shared.update(_consts())
    xp = f(x_prompt)
    xs = f(x_sample)
    sc = f(state_conv)[0]
    sCc = f(state_C)[0]
    sn = f(state_n)[0]
    sm = f(state_m)[0]
    in_maps = []
    for c in range(NCORE):
        sl = slice(c * NSEQ, (c + 1) * NSEQ)
        xsc = xs[sl].reshape(64, D)
        m = dict(shared)
        m.update(
            xT_p=f(xp[c].T.reshape(8, 128, SEQ).transpose(1, 0, 2)),
            x_p=xp[c],
            xT_s=f(xsc.T.reshape(8, 128, 64).transpose(1, 0, 2)),
            x_s=xsc,
            sconvT=f(sc[sl].reshape(NSEQ, 2, 8, 128).transpose(3, 2, 0, 1)),
            sC=sCc[sl],
            snT=f(sn[sl].reshape(NSEQ, NH, 2, 128).transpose(3, 2, 0, 1)),
            smT=f(sm[sl].T),
        )
        in_maps.append(m)
    return in_maps


_NC_CACHE = {}


def kernel(**inputs):
    in_maps = prep_inputs(**inputs)
    if "nc" not in _NC_CACHE:
        _NC_CACHE["nc"] = build_nc()[0]
    nc = _NC_CACHE["nc"]
    res = run_bass_kernel_spmd(nc, in_maps, core_ids=list(range(NCORE)))
    R = res.results
    y_prompt = np.stack([R[c]["y_p"] for c in range(NCORE)], 0)
    y_sample = np.concatenate([R[c]["y_s"].reshape(NSEQ, DEC, D) for c in range(NCORE)], 0)
    conv_prompt = np.stack([R[c]["conv_p"].transpose(2, 1, 0).reshape(2, D) for c in range(NCORE)], 0)[None]
    conv_sample = np.concatenate([R[c]["conv_s"].transpose(2, 3, 1, 0).reshape(NSEQ, 2, D) for c in range(NCORE)], 0)[None]
    C_prompt = np.stack([R[c]["C_p"] for c in range(NCORE)], 0)[None]
    C_sample = np.concatenate([R[c]["C_s"] for c in range(NCORE)], 0)[None]
    n_prompt = np.stack([R[c]["n_p"].transpose(1, 2, 0).reshape(NH, DH) for c in range(NCORE)], 0)[None]
    n_sample = np.concatenate([R[c]["n_s"].transpose(1, 2, 3, 0).reshape(NSEQ, NH, DH) for c in range(NCORE)], 0)[None]
    m_prompt = np.stack([R[c]["m_p"].reshape(NH) for c in range(NCORE)], 0)[None]
    m_sample = np.concatenate([R[c]["m_s"].T for c in range(NCORE)], 0)[None]
    outs = (y_prompt, y_sample, conv_prompt, conv_sample, C_prompt, C_sample, n_prompt, n_sample, m_prompt, m_sample)
    return tuple(np.ascontiguousarray(o, dtype=np.float32) for o in outs)
```

```python
import os
from contextlib import ExitStack
import numpy as np
import concourse.bass as bass
import concourse.mybir as mybir
from concourse.bass_utils import run_bass_kernel_spmd

F32 = mybir.dt.float32
BF16 = mybir.dt.bfloat16
AF = mybir.ActivationFunctionType
ALU = mybir.AluOpType
AX = mybir.AxisListType

D = 1024
SEQ = 2048
NCORE = 8
NSEQ = 16
DEC = 4
NH = 4
DH = 256
LN_EPS = 1e-5
ALPHA = 2.0 ** 0.25
NEG = -1.0e4
KSCALE = DH ** -0.5

SAME_ENGINE_SYNC = True
SAME_ENGINE_RAW_ONLY = False
LIST_SCHED = True
SCHED_XLAT = 0.5
ACC_ENG = 'scalar'
HB_MODE = 'pool'
LN_FUSE = True
ST_ENG = 'sync'
CS_ST_ENG = 'sync'
FF_GRP = 512
SCHED_EPS = 0.3
LN2B_ENG = ('gpsimd',)
LN1B_ENG = ('gpsimd',)
MASK_MM = False
CBF_ENG = 'vector'
VTOK_ENG = 'scalar'
NPM = 2
KTOK_TRANSPOSE = False
NB3 = 3
NYT = 3
LN2G_ENG = 'vector'
GATES_OVERLAP = True
SQ_ENG = ('gpsimd',)
NHID = 2
LNB_ENG = 'gpsimd'
X1T_ENG = 'scalar'
NC0X = 7
SCHED_WINDOW = 400
SCHED_SLAT = 0.15
CAST_PATTERN = ("scalar", "scalar", "scalar", "vector")


class _Op:
    __slots__ = ("eng", "fns", "reads", "writes", "deps", "is_dma", "semkey", "idx", "stage", "raw", "t_start", "t_fin", "crit", "t_ready",
                 "need_inc", "inc_val", "barrier", "is_barrier", "b_last", "b_dma")

    def __init__(self, eng, fns, reads, writes, is_dma, semkey):
        self.eng = eng
        self.fns = fns
        self.reads = reads
        self.writes = writes
        self.deps = set()
        self.is_dma = is_dma
        self.semkey = semkey
        self.need_inc = False
        self.inc_val = None
        self.barrier = None
        self.is_barrier = False


class Prog:
    ENGS = ("sync", "scalar", "vector", "gpsimd", "tensor")

    def __init__(self, nc):
        self.nc = nc
        self.ops = []
        self.last_writer = {}
        self.readers = {}
        self.barrier_op = None
        self.enabled = True
        self.stage = "init"

    def _record(self, op):
        op.stage = self.stage
        if not self.enabled:
            return op
        pr = tuple(r for r in op.reads if isinstance(r, tuple) and r and r[0] == "pb")
        if pr:
            op.reads = tuple(r for r in op.reads if r not in pr)
            op.writes = tuple(op.writes) + tuple(r for r in pr if r not in op.writes)
        op.idx = len(self.ops)
        deps = set()
        for r in op.reads:
            w = self.last_writer.get(r)
            if w is not None:
                deps.add(w)
        op.raw = set(deps)
        for r in op.writes:
            w = self.last_writer.get(r)
            if w is not None:
                deps.add(w)
            for rd in self.readers.get(r, ()):
                deps.add(rd)
        deps.discard(op.idx)
        op.deps = deps
        op.barrier = self.barrier_op
        for r in op.writes:
            self.last_writer[r] = op.idx
            self.readers[r] = []
        for r in op.reads:
            if r not in op.writes:
                self.readers.setdefault(r, []).append(op.idx)
        self.ops.append(op)
        return op

    def op(self, eng, fns, reads=(), writes=()):
        if isinstance(fns, tuple):
            fns = [fns]
        return self._record(_Op(eng, list(fns), tuple(reads), tuple(writes), False, None))

    def dma(self, eng, fn, reads=(), writes=(), semkey=None):
        assert semkey is not None
        return self._record(_Op(eng, [fn], tuple(reads), tuple(writes), True, semkey))

    def barrier(self):
        if not self.enabled:
            return
        b = _Op("sync", [], (), (), False, None)
        b.is_barrier = True
        b.raw = set()
        self._record(b)
        self.last_writer = {}
        self.readers = {}
        self.barrier_op = b.idx

    def _skip_same(self, dop, o):
        if dop.eng == o.eng and not o.is_dma and not dop.is_dma:
            if dop.eng == "tensor" or not SAME_ENGINE_SYNC:
                return True
            if SAME_ENGINE_RAW_ONLY and dop.idx not in o.raw:
                return True
        return False


    @staticmethod
    def _free(ap):
        sh = ap.shape
        n = 1
        for d in sh[1:]:
            n *= int(d)
        return n, int(sh[0])

    def _cost(self, o):
        if o.is_dma:
            meth, args, kw = o.fns[0]
            out = kw["out"]
            n, p = self._free(out)
            esz = 2 if out.dtype == BF16 else 4
            return 0.15, 2.0 + n * p * esz / 300e3
        t = 0.0
        for meth, args, kw in o.fns:
            out = kw.get("out", args[0] if args else None)
            n, p = self._free(out)
            if o.eng == "tensor":
                if meth == "matmul":
                    c = max(n, 64) / 2300.0
                    if kw["lhsT"].dtype == F32:
                        c *= 4
                    t += c + 0.01
                else:
                    t += 0.12
            elif o.eng == "scalar":
                t += 0.12 + n / 1000.0
            elif o.eng == "vector":
                t += 0.12 + n / 960.0
            else:
                t += 0.3 + n * 2.3 / 1000.0
        return t, 0.0

    def schedule(self):
        ops = self.ops
        XLAT = SCHED_XLAT
        SLAT = SCHED_SLAT
        order = []
        t_eng = {e: 0.0 for e in self.ENGS}
        dma_pipe = [0.0]
        fin = {}
        regions = []
        cur = []
        for o in ops:
            if o.is_barrier:
                regions.append((cur, o))
                cur = []
            else:
                cur.append(o)
        regions.append((cur, None))
        for reg, bar in regions:
            if reg:
                ids = set(o.idx for o in reg)
                cost = {o.idx: self._cost(o) for o in reg}
                succ = {o.idx: [] for o in reg}
                npred = {}
                for o in reg:
                    ds = [d for d in o.deps if d in ids]
                    npred[o.idx] = len(ds)
                    for d in ds:
                        succ[d].append(o.idx)
                bl = {}
                for o in reversed(reg):
                    m = 0.0
                    for s_ in succ[o.idx]:
                        m = max(m, bl[s_] + (XLAT if ops[s_].eng != o.eng else SLAT))
                    bl[o.idx] = m + cost[o.idx][0] + cost[o.idx][1]
                t0 = max(t_eng.values())
                for e in t_eng:
                    t_eng[e] = t0
                _busy = {e: 0.0 for e in self.ENGS}
                crit = {}
                ready = {e: [] for e in self.ENGS}
                rtime = {}
                for o in reg:
                    if npred[o.idx] == 0:
                        ready[o.eng].append(o.idx)
                        rtime[o.idx] = t0
                nleft = len(reg)
                WINDOW = SCHED_WINDOW
                base_pos = reg[0].idx
                done_cnt = 0
                sched_flags = {}
                lowest_unsched = reg[0].idx
                reg_idx = [o.idx for o in reg]
                ptr = 0
                while nleft:
                    while ptr < len(reg_idx) and reg_idx[ptr] in sched_flags:
                        ptr += 1
                    low = reg_idx[ptr] if ptr < len(reg_idx) else None
                    best = None
                    for e in self.ENGS:
                        if not ready[e]:
                            continue
                        cands = [i for i in ready[e] if low is None or i - low <= WINDOW]
                        if not cands:
                            continue
                        est = [(max(t_eng[e], rtime[i]), i) for i in cands]
                        mn = min(x[0] for x in est)
                        near = [i for (st, i) in est if st <= mn + SCHED_EPS]
                        pick = max(near, key=lambda i: (bl[i], -i))
                        key = (mn, -bl[pick], pick)
                        if best is None or key < best[0]:
                            best = (key, e, pick)
                    if best is None:
                        allc = [(i, e) for e in self.ENGS for i in ready[e]]
                        i, e = min(allc)
                        best = ((max(t_eng[e], rtime[i]), 0, i), e, i)
                    (_, e, i) = best
                    o = ops[i]
                    st = max(t_eng[e], rtime[i])
                    occ, lat = cost[i]
                    if o.is_dma:
                        t_eng[e] = st + occ
                        s2 = max(st, dma_pipe[0])
                        f = s2 + lat
                        dma_pipe[0] = s2 + (lat - 2.0)
                    else:
                        t_eng[e] = st + occ
                        f = st + occ
                    fin[i] = f
                    o.t_start = st
                    o.t_fin = f
                    o.crit = crit.get(i)
                    o.t_ready = rtime[i]
                    _busy[e] += occ
                    ready[e].remove(i)
                    sched_flags[i] = True
                    order.append(o)
                    nleft -= 1
                    for s_ in succ[i]:
                        npred[s_] -= 1
                        lt = XLAT if ops[s_].eng != e or o.is_dma else SLAT
                        if f + lt > rtime.get(s_, t0):
                            crit[s_] = i
                        rtime[s_] = max(rtime.get(s_, t0), f + lt)
                        if npred[s_] == 0:
                            ready[ops[s_].eng].append(s_)
            if reg:
                if not hasattr(self, "report"):
                    self.report = []
                self.report.append((reg[0].stage, reg[-1].stage, t0, max(max(t_eng.values()), max(fin[o.idx] for o in reg)), dict(_busy), len(reg)))
            if bar is not None:
                order.append(bar)
        self.est_time = max(t_eng.values())
        return order

    def emit(self, sem_ctx):
        nc = self.nc
        ops = self.ops
        order = self.schedule() if LIST_SCHED else list(ops)
        self.order = order
        last_c = {}
        for o in order:
            if o.is_barrier:
                o.b_last = dict(last_c)
                for e, i in last_c.items():
                    ops[i].need_inc = True
                continue
            for d in o.deps:
                dop = ops[d]
                if dop.is_dma or self._skip_same(dop, o):
                    continue
                dop.need_inc = True
            if not o.is_dma and o.fns:
                last_c[o.eng] = o.idx
        eng_cnt = {e: 0 for e in self.ENGS}
        dma_cnt = {}
        for o in order:
            if o.is_dma:
                dma_cnt[o.semkey] = dma_cnt.get(o.semkey, 0) + 1
            elif o.need_inc:
                eng_cnt[o.eng] += 1
                o.inc_val = eng_cnt[o.eng]
        eng_sem = {e: sem_ctx("e_" + e) for e in self.ENGS if eng_cnt[e] > 0}
        dma_sem = {k: sem_ctx("d_%d" % i) for i, k in enumerate(dma_cnt)}
        self.n_sems = len(eng_sem) + len(dma_sem)
        per_eng = {e: [] for e in self.ENGS}
        running = {}
        snapshots = {}
        for o in order:
            if o.is_barrier:
                o.b_dma = dict(running)
                continue
            if o.is_dma:
                running[o.semkey] = running.get(o.semkey, 0) + 1
            per_eng[o.eng].append(o)
            need = {}
            if o.barrier is not None:
                b = ops[o.barrier]
                for e, i in b.b_last.items():
                    if e == "tensor" and o.eng == "tensor" and not o.is_dma:
                        continue
                    need[("e", e)] = ops[i].inc_val
                for k, c in b.b_dma.items():
                    need[("d", k)] = 16 * c
            for d in o.deps:
                dop = ops[d]
                if dop.is_dma:
                    cnt = running[dop.semkey]
                    if o.is_dma and o.semkey == dop.semkey:
                        cnt -= 1
                    key = ("d", dop.semkey)
                    need[key] = max(need.get(key, 0), 16 * cnt)
                else:
                    if self._skip_same(dop, o):
                        continue
                    key = ("e", dop.eng)
                    need[key] = max(need.get(key, 0), dop.inc_val)
            snapshots[o.idx] = need
        final_dma = dict(running)

        def run_engine(ename, engobj, final_waiter):
            waited = {}
            for o in per_eng[ename]:
                for key, val in snapshots[o.idx].items():
                    if val <= 0 or waited.get(key, 0) >= val:
                        continue
                    sem = dma_sem[key[1]] if key[0] == "d" else eng_sem[key[1]]
                    if os.environ.get("PROG_DEBUG"):
                        print("WAIT", ename, o.idx, [f[0] for f in o.fns][:1], key, val)
                    engobj.wait_ge(sem, val)
                    waited[key] = val
                n = len(o.fns)
                for i, fn in enumerate(o.fns):
                    ins = getattr(engobj, fn[0])(*fn[1], **fn[2])
                    if i == n - 1:
                        if o.is_dma:
                            ins.then_inc(dma_sem[o.semkey], 16)
                        elif o.need_inc:
                            ins.then_inc(eng_sem[o.eng], 1)
            if final_waiter:
                for k, cnt in final_dma.items():
                    if waited.get(("d", k), 0) < 16 * cnt:
                        engobj.wait_ge(dma_sem[k], 16 * cnt)

        with nc.Block() as block:
            @block.sync
            def _(eng):
                run_engine("sync", eng, True)

            @block.scalar
            def _(eng):
                run_engine("scalar", eng, False)

            @block.vector
            def _(eng):
                run_engine("vector", eng, False)

            @block.gpsimd
            def _(eng):
                run_engine("gpsimd", eng, False)

            @block.tensor
            def _(eng):
                run_engine("tensor", eng, False)


def I(meth, *args, **kw):
    return (meth, args, kw)


class Rot:
    def __init__(self, items):
        self.items = list(items)
        self.i = 0

    def next(self):
        it = self.items[self.i % len(self.items)]
        self.i += 1
        return it


def blk_tiles(blk):
    t = [(i * 128, 128) for i in range(8)]
    if blk == 1:
        t.append((1024, 64))
    return t


def blk_supers(blk):
    s = [(0, 512), (512, 512)]
    if blk == 1:
        s.append((1024, 64))
    return s


def tks(lo, n):
    return list(range(lo // 128, (lo + n - 1) // 128 + 1))


def build_nc(debug=(), stop_after=None, hstop=None):
    nc = bass.Bass("TRN2", target_bir_lowering=False)

    def din(name, shape):
        return nc.dram_tensor(name, list(shape), F32, kind="ExternalInput").ap()

    def dout(name, shape):
        return nc.dram_tensor(name, list(shape), F32, kind="ExternalOutput").ap()

    xT_p = din("xT_p", [128, 8, SEQ])
    x_p = din("x_p", [SEQ, D])
    xT_s = din("xT_s", [128, 8, 64])
    x_s = din("x_s", [64, D])
    sconvT = din("sconvT", [128, 8, NSEQ, 2])
    sC = din("sC", [NSEQ, NH, DH, DH])
    snT = din("snT", [128, 2, NSEQ, NH])
    smT = din("smT", [NH, NSEQ])
    w_in_l = din("w_in_l", [36, 128, 8, 256])
    w_g = din("w_g", [128, 8, 8])
    bgate = din("bgate", [4, 2])
    convw = din("convw", [128, 8, 3])
    wco_l = din("wco_l", [4, 128, 8, 256])
    mhg_l = din("mhg_l", [128, 8])
    wmo_l = din("wmo_l", [4, 128, 8, 256])
    wo_l = din("wo_l", [4, 128, 8, 256])
    ln1g = din("ln1g", [1, D])
    ln1b = din("ln1b", [1, D])
    wff1_l = din("wff1_l", [16, 128, 8, 256])
    wff2_l = din("wff2_l", [8, 2, 128, 4, 512])
    ln2g = din("ln2g", [1, D])
    ln2b = din("ln2b", [1, D])
    c_ident = din("c_ident", [128, 128])
    c_masks = din("c_masks", [128, 192])
    c_sel4 = din("c_sel4", [4, 4, 128])
    c_selm = din("c_selm", [4, 4, 16])
    c_bmask = din("c_bmask", [128, 16, 64])
    c_rmask = din("c_rmask", [64, 16])
    c_r01 = din("c_r01", [4, 2, 64])
    y_p = dout("y_p", [SEQ, D])
    y_s = dout("y_s", [64, D])
    conv_p = dout("conv_p", [128, 8, 2])
    conv_s = dout("conv_s", [128, 8, NSEQ, 2])
    C_p = dout("C_p", [NH, DH, DH])
    C_s = dout("C_s", [NSEQ, NH, DH, DH])
    n_p = dout("n_p", [128, NH, 2])
    n_s = dout("n_s", [128, NSEQ, NH, 2])
    m_p = dout("m_p", [NH, 1])
    m_s = dout("m_s", [NH, NSEQ])

    P = Prog(nc)
    dbg_outs = {}

    with ExitStack() as top:
        uid = [0]

        def sb(es, name, shape, dt=F32):
            uid[0] += 1
            return es.enter_context(nc.sbuf_tensor("%s_u%d" % (name, uid[0]), list(shape), dt))

        def V(fn, r=(), w=()):
            return P.op("vector", fn, r, w)

        def A(fn, r=(), w=()):
            return P.op("scalar", fn, r, w)

        def G(fn, r=(), w=()):
            return P.op("gpsimd", fn, r, w)

        def T(fn, r=(), w=()):
            return P.op("tensor", fn, r, w)

        def LD(out, in_, w, semkey, r=()):
            return P.dma("sync", I("dma_start", out=out, in_=in_), reads=r, writes=w, semkey=semkey)

        def ST(out, in_, r, semkey, w=(), eng=None):
            return P.dma(eng or ST_ENG, I("dma_start", out=out, in_=in_), reads=r, writes=w, semkey=semkey)

        def dbg(name, ap, keys, shape):
            if name in debug:
                d = nc.dram_tensor("dbg_" + name, list(shape), ap.dtype, kind="ExternalOutput").ap()
                dbg_outs[name] = d
                ST(d, ap, keys, ("dbg", name))

        pbs = [top.enter_context(nc.psum_tensor("pb%d" % i, [128, 512], F32)) for i in range(8)]
        PB = [("pb", i) for i in range(8)]
        pT = pbs[7][:].bitcast(BF16)
        KPT = PB[7]

        identf = sb(top, "identf", [128, 128])
        identb = sb(top, "identb", [128, 128], BF16)
        masks = sb(top, "masks", [128, 192])
        sel4 = sb(top, "sel4", [4, 4, 128])
        selm = sb(top, "selm", [4, 4, 16])
        bmask = sb(top, "bmask", [128, 16, 64], BF16)
        bmaskf = None
        rmask = sb(top, "rmask", [64, 16])
        r01 = sb(top, "r01", [4, 2, 64])
        ones4 = sb(top, "ones4", [4, 128])
        cw = sb(top, "cw", [128, 8, 3])
        mhg = sb(top, "mhg", [128, 8])
        bg_t = sb(top, "bg_t", [4, 2])
        nbf = sb(top, "nbf", [4, 1])
        m0T = sb(top, "m0T", [4, NSEQ])
        snT_t = sb(top, "snT_t", [128, 2, NSEQ, NH])
        scv = sb(top, "scv", [128, 8, NSEQ, 2])
        Cst = sb(top, "Cst", [128, NH, 2, 257])
        Cbf = sb(top, "Cbf", [128, NH, 2, 257], BF16)
        carry = sb(top, "carry", [4, 2])
        convcar = sb(top, "convcar", [128, 8, 2])
        convout = sb(top, "convout", [128, 8, 2])
        convsout = sb(top, "convsout", [128, 8, NSEQ, 2])
        nsout = sb(top, "nsout", [128, NSEQ, NH, 2])
        npout = sb(top, "npout", [128, NH, 2])
        msout = sb(top, "msout", [4, 17])
        NWS = 3
        NWB = 8
        wst = [sb(top, "wst%d" % i, [128, 8, 256]) for i in range(NWS)]
        wbt = [sb(top, "wbt%d" % i, [128, 8, 256], BF16) for i in range(NWB)]
        wst_rot = Rot(range(NWS))
        wb_free = list(range(NWB))

        LD(identf[:], c_ident, ["identf"], "identf")
        LD(masks[:], c_masks, ["masks"], "masks")
        LD(sel4[:], c_sel4, ["sel4"], "sel4")
        LD(selm[:], c_selm, ["selm"], "selm")
        LD(rmask[:], c_rmask, ["rmask"], "rmask")
        LD(r01[:], c_r01, ["r01"], "r01")
        LD(cw[:], convw, ["cw"], "cw")
        LD(mhg[:], mhg_l, ["mhg"], "mhg")
        LD(bg_t[:], bgate, ["bg_t"], "bg_t")
        LD(m0T[:], smT, ["m0T"], "m0T")
        LD(snT_t[:], snT, ["snT"], "snT")
        LD(scv[:], sconvT, ["scv"], "scv")
        with ExitStack() as es0:
            bmf = sb(es0, "bmf", [128, 16, 64])
            LD(bmf[:], c_bmask, ["bmf"], "bmf")
            V(I("tensor_copy", out=bmask[:], in_=bmf[:]), ["bmf"], ["bmask"])
            V(I("tensor_copy", out=identb[:], in_=identf[:]), ["identf"], ["identb"])
            V(I("tensor_scalar", out=nbf[:], in0=bg_t[:, 1:2], scalar1=-1.0, scalar2=None, op0=ALU.mult), ["bg_t"], ["nbf"])
            V(I("memset", ones4[:], 1.0), [], ["ones4"])
            V(I("memset", Cst[:], 0.0), [], ["Cst"])
            V(I("memset", Cbf[:], 0.0), [], ["Cbf"])
            V(I("memset", carry[:], 0.0), [], ["carry"])
            V(I("memset", convcar[:], 0.0), [], ["convcar"])
            P.barrier()

        def w_load(dram_blk, scale_mhg=False, view4=False):
            si = wst_rot.next()
            assert wb_free, "weight pool exhausted"
            bi = wb_free.pop(0)
            stg, wb = wst[si], wbt[bi]
            if view4:
                dst = stg[:].rearrange("p a b -> p (a b)").rearrange("p (k c) -> p k c", k=4)
            else:
                dst = stg[:]
            LD(dst, dram_blk, [("wst", si)], ("wst", si))
            ce = CAST_PATTERN[cast_rr[0] % len(CAST_PATTERN)]
            cast_rr[0] += 1
            if scale_mhg:
                if ce == "gpsimd":
                    ce = "vector"
                if ce == "scalar":
                    P.op("scalar", [I("activation", out=wb[:, k, :], in_=stg[:, k, :], func=AF.Copy, scale=mhg[:, k:k + 1])
                                    for k in range(8)], [("wst", si), "mhg"], [("wb", bi)])
                else:
                    P.op("vector", [I("tensor_scalar", out=wb[:, k, :], in0=stg[:, k, :], scalar1=mhg[:, k:k + 1],
                                      scalar2=None, op0=ALU.mult) for k in range(8)],
                         [("wst", si), "mhg"], [("wb", bi)])
            elif ce == "scalar":
                A(I("activation", out=wb[:], in_=stg[:], func=AF.Copy), [("wst", si)], [("wb", bi)])
            elif ce == "vector":
                V(I("tensor_copy", out=wb[:], in_=stg[:]), [("wst", si)], [("wb", bi)])
            else:
                G(I("tensor_copy", out=wb[:], in_=stg[:]), [("wst", si)], [("wb", bi)])
            if view4:
                ap = wb[:].rearrange("p a b -> p (a b)").rearrange("p (k c) -> p k c", k=4)
            else:
                ap = wb[:]
            return ap, ("wb", bi), bi

        pre = {}

        def w_get(key, dram_blk, **kw):
            if key in pre:
                return pre.pop(key)
            return w_load(dram_blk, **kw)

        def w_prefetch(items):
            for key, dram_blk, kw in items:
                pre[key] = w_load(dram_blk, **kw)

        def head_items(h):
            return [(("w_in", 12 + h), w_in_l[12 + h], {}), (("w_in", 16 + h), w_in_l[16 + h], {}),
                    (("w_in", 20 + h), w_in_l[20 + h], {}), (("w_in", 24 + h), w_in_l[24 + h], {})]

        def conv_items(j):
            return [(("w_in", j), w_in_l[j], {}), (("w_in", 4 + j), w_in_l[4 + j], {}), (("w_in", 8 + j), w_in_l[8 + j], {})]

        def merge_items(j):
            return [(("wco", j), wco_l[j], {}), (("wmo", j), wmo_l[j], dict(scale_mhg=True)),
                    (("w_in", 28 + j), w_in_l[28 + j], {}), (("w_in", 32 + j), w_in_l[32 + j], {})]

        def wo_items():
            return [(("wo", i), wo_l[i], {}) for i in range(4)]

        def ffn_items(g):
            return [(("ff1", 2 * g), wff1_l[2 * g], {}), (("ff1", 2 * g + 1), wff1_l[2 * g + 1], {}),
                    (("ff2", g, 0), wff2_l[g, 0], dict(view4=True)), (("ff2", g, 1), wff2_l[g, 1], dict(view4=True))]

        def get_items(items):
            return tuple(w_get(k, b_, **kw) for k, b_, kw in items)

        def w_free(*slots):
            for s in slots:
                wb_free.append(s)

        stage_count = [0]
        cast_rr = [0]

        def stop_here(tag):
            return stop_after is not None and tag == stop_after

        stopped = False
        for blk in range(2):
            if stopped:
                break
            tiles = blk_tiles(blk)
            supers = blk_supers(blk)
            NT = 1024 if blk == 0 else 1088
            NTI = len(tiles)
            with ExitStack() as esM:
                mergedT_raw = sb(esM, "mergedT%d" % blk, [128, 4 * NT], F32)
                mergedT = mergedT_raw[:].bitcast(BF16).rearrange("p (a b) -> p a b", a=8)
                with ExitStack() as esA:
                    xT = sb(esA, "xT%d" % blk, [128, 8, NT], BF16)
                    ucT_raw = sb(esA, "ucT%d" % blk, [128, 4 * NT], F32)
                    ucT = ucT_raw[:].bitcast(BF16).rearrange("p (a b) -> p a b", a=8)
                    hnT = sb(esA, "hnT%d" % blk, [128, 8, NT], BF16)
                    a_row = sb(esA, "a_row%d" % blk, [4, NT])
                    tokm = sb(esA, "tokm%d" % blk, [128, NTI, 16])
                    decb = sb(esA, "decb%d" % blk, [128, 4, 24])

                    P.stage = 'b%d_s0_x' % blk
                    for pi, lo in enumerate(range(0, 1024, 256)):
                        si = wst_rot.next()
                        LD(wst[si][:], xT_p[:, :, blk * 1024 + lo: blk * 1024 + lo + 256], [("wst", si)], ("wst", si))
                        A(I("activation", out=xT[:, :, lo:lo + 256], in_=wst[si][:], func=AF.Copy),
                          [("wst", si)], [("xT", t) for t in tks(lo, 256)])
                    if blk == 1:
                        si = wst_rot.next()
                        LD(wst[si][:, :, 0:64], xT_s, [("wst", si)], ("wst", si))
                        A(I("activation", out=xT[:, :, 1024:1088], in_=wst[si][:, :, 0:64], func=AF.Copy),
                          [("wst", si)], [("xT", 8)])
                    dbg("xT%d" % blk, xT[:], [("xT", t) for t in range(NTI)], [128, 8, NT])

                    P.stage = 'b%d_s1a_gates' % blk
                    with ExitStack() as esG:
                        GK = ["gates"]
                        wgs = sb(esA, "wgs", [128, 8, 8])
                        wgb = sb(esA, "wgb", [128, 8, 8], BF16)
                        li = ucT_raw[0:4, 0:NT]
                        lf = ucT_raw[0:4, NT:2 * NT]
                        Bc = ucT_raw[0:4, 2 * NT:3 * NT]
                        Mr = ucT_raw[0:4, 3 * NT:4 * NT]
                        tmp = mergedT_raw[0:4, 0:NT]
                        zer = mergedT_raw[0:4, NT:NT + 1024]
                        mprev = sb(esA, "mprev", [4, 24])
                        mend = sb(esA, "mend", [4, 24])
                        dec = sb(esA, "dec", [4, 24])
                        decx = sb(esA, "decx", [4, 4, 24])
                        LD(wgs[:], w_g, ["wgs"], "wgs")
                        V(I("tensor_copy", out=wgb[:], in_=wgs[:]), ["wgs"], ["wgb"])
                        V(I("memset", zer[:], 0.0), [], ["zer"])
                        V(I("memset", decx[:], 0.0), [], GK)
                        V(I("memset", dec[:], 0.0), [], GK)
                        V(I("memset", mprev[:], 0.0), [], GK)
                        V(I("memset", mend[:], 0.0), [], GK)
                        for si_, (lo, n) in enumerate(supers):
                            b0, b1 = (0, 1) if si_ % 2 == 0 else (2, 3)
                            T([I("matmul", pbs[b0][0:4, 0:n], lhsT=wgb[:, k, 0:4], rhs=xT[:, k, lo:lo + n],
                                                                          start=(k == 0), stop=(k == 7)) for k in range(8)],
                              ["wgb"] + [("xT", t) for t in tks(lo, n)], [PB[b0]])
                            T([I("matmul", pbs[b1][0:4, 0:n], lhsT=wgb[:, k, 4:8], rhs=xT[:, k, lo:lo + n],
                                                                          start=(k == 0), stop=(k == 7)) for k in range(8)],
                              ["wgb"] + [("xT", t) for t in tks(lo, n)], [PB[b1]])
                            A(I("activation", out=li[:, lo:lo + n], in_=pbs[b0][0:4, 0:n], func=AF.Identity,
                                                                        bias=bg_t[:, 0:1], scale=1.0),
                              [PB[b0], "bg_t"], [("li", lo)])
                            A(I("activation", out=tmp[:, lo:lo + n], in_=pbs[b1][0:4, 0:n], func=AF.Exp,
                                                                        bias=nbf[:, 0:1], scale=-1.0),
                              [PB[b1], "nbf"], [("gtmp", lo)])
                            A(I("activation", out=lf[:, lo:lo + n], in_=tmp[:, lo:lo + n], func=AF.Ln, bias=1.0, scale=1.0),
                              [("gtmp", lo)], [("lf", lo)])
                        allg = [("li", lo) for lo, n in supers] + [("lf", lo) for lo, n in supers] + [("gtmp", lo) for lo, n in supers]
                        V(I("tensor_scalar", out=lf[:], in0=lf[:], scalar1=-1.0, scalar2=None, op0=ALU.mult), allg + ["zer"], GK)
                        init_b = 0.0 if blk == 0 else carry[:, 0:1]
                        init_m = 0.0 if blk == 0 else carry[:, 1:2]
                        V(I("tensor_tensor_scan", out=Bc[:, 0:1024], data0=zer[:], data1=lf[:, 0:1024], initial=init_b,
                                                         op0=ALU.add, op1=ALU.add), GK + ["carry"], GK)
                        V(I("tensor_tensor", out=a_row[:, 0:1024], in0=li[:, 0:1024], in1=Bc[:, 0:1024], op=ALU.subtract), GK, GK + ["a_row"])
                        V(I("tensor_tensor_scan", out=Mr[:, 0:1024], data0=zer[:], data1=a_row[:, 0:1024], initial=init_m,
                                                         op0=ALU.add, op1=ALU.max), GK + ["carry", "a_row"], GK)
                        if blk == 0:
                            V(I("memset", mprev[:, 0:1], 0.0), GK, GK)
                        else:
                            V(I("tensor_copy", out=mprev[:, 0:1], in_=carry[:, 1:2]), GK + ["carry"], GK)
                        V(I("tensor_copy", out=mprev[:, 1:8], in_=Mr[:, 127:1023:128][:, 0:7]), GK, GK)
                        V(I("tensor_copy", out=mend[:, 0:8], in_=Mr[:, 127:1024:128]), GK, GK)
                        if blk == 0:
                            V(I("tensor_copy", out=carry[:, 0:1], in_=Bc[:, 1023:1024]), GK, GK + ["carry"])
                            V(I("tensor_copy", out=carry[:, 1:2], in_=Mr[:, 1023:1024]), GK, GK + ["carry"])
                        if blk == 1:
                            s3 = lambda t: t[:, 1024:1088].rearrange("p (b t) -> p b t", t=4)
                            V(I("tensor_tensor_scan", out=Bc[:, 1024:1088], data0=r01[:, 0, :], data1=lf[:, 1024:1088], initial=0.0,
                                                             op0=ALU.mult, op1=ALU.add), GK + ["r01"], GK)
                            V(I("tensor_tensor", out=a_row[:, 1024:1088], in0=li[:, 1024:1088], in1=Bc[:, 1024:1088], op=ALU.subtract),
                              GK, GK + ["a_row"])
                            V(I("tensor_copy", out=tmp[:, 1024:1088], in_=a_row[:, 1024:1088]), GK + ["a_row"], GK)
                            V(I("tensor_tensor", out=s3(tmp)[:, :, 0:1], in0=s3(a_row)[:, :, 0:1], in1=m0T[:].unsqueeze(2), op=ALU.max),
                              GK + ["a_row", "m0T"], GK)
                            V(I("tensor_tensor_scan", out=Mr[:, 1024:1088], data0=r01[:, 1, :], data1=tmp[:, 1024:1088], initial=0.0,
                                                             op0=ALU.add, op1=ALU.max), GK + ["r01"], GK)
                            V(I("tensor_copy", out=mprev[:, 8:24], in_=m0T[:]), GK + ["m0T"], GK)
                            V(I("tensor_copy", out=mend[:, 8:24], in_=Mr[:, 1027:1088:4]), GK, GK)
                        ncol = 8 if blk == 0 else 24
                        V(I("tensor_tensor", out=dec[:, 0:ncol], in0=mprev[:, 0:ncol], in1=mend[:, 0:ncol], op=ALU.subtract), GK, GK)
                        A(I("activation", out=dec[:, 0:ncol], in_=dec[:, 0:ncol], func=AF.Exp), GK, GK)
                        V(I("tensor_tensor", out=decx[:, :, 0:ncol], in0=dec[:, 0:ncol].unsqueeze(1).to_broadcast([4, 4, ncol]),
                                                    in1=selm[:, :, 0:1].to_broadcast([4, 4, ncol]), op=ALU.mult), GK + ["selm"], GK)
                        T(I("matmul", pbs[4][:, 0:4 * 24], lhsT=ones4[:], rhs=decx[:].rearrange("p a b -> p (a b)"), start=True, stop=True),
                          GK + ["ones4"], [PB[4]])
                        V(I("tensor_copy", out=decb[:].rearrange("p a b -> p (a b)"), in_=pbs[4][:, 0:96]), [PB[4]], ["decb"])
                        def views(t, lo, n):
                            return t[:, lo:lo + n]
                        quant = []
                        def transpose_rows(src, qi):
                            T([I("transpose", out=pbs[5][0:n, ti * 16 + qi * 4: ti * 16 + (qi + 1) * 4], in_=src[:, lo:lo + n],
                                 identity=identf[0:4, 0:4]) for ti, (lo, n) in enumerate(tiles)],
                              GK + ["identf"], [PB[5]])
                        def seg3(t, blkpart):
                            if blkpart == "p":
                                return t[:, 0:1024].rearrange("p (c t) -> p c t", t=128)
                            return t[:, 1024:1088].rearrange("p (b t) -> p b t", t=4)
                        def bc3(t, blkpart):
                            if blkpart == "p":
                                return t[:, 0:8].unsqueeze(2).to_broadcast([4, 8, 128])
                            return t[:, 8:24].unsqueeze(2).to_broadcast([4, 16, 4])
                        parts = ["p"] + (["s"] if blk == 1 else [])
                        V(I("tensor_scalar", out=tmp[:], in0=Mr[:], scalar1=-1.0, scalar2=None, op0=ALU.mult), GK, GK)
                        transpose_rows(tmp, 0)
                        for bp in parts:
                            V(I("tensor_tensor", out=seg3(tmp, bp), in0=bc3(mprev, bp), in1=seg3(Mr, bp), op=ALU.subtract),
                              GK + [PB[5]] + [("tokm", t) for t in range(NTI)], GK)
                        A(I("activation", out=tmp[:], in_=tmp[:], func=AF.Exp), GK, GK)
                        transpose_rows(tmp, 1)
                        V(I("tensor_tensor", out=tmp[:], in0=Bc[:], in1=Mr[:], op=ALU.add), GK + [PB[5]] + [("tokm", t) for t in range(NTI)], GK)
                        if blk == 1:
                            V(I("tensor_copy", out=msout[:, 0:1], in_=tmp[:, 1023:1024]), GK, ["msout"])
                            V(I("tensor_copy", out=msout[:, 1:17], in_=tmp[:, 1027:1088:4]), GK, ["msout"])
                            ST(m_p, msout[:, 0:1], ["msout"], "m_p")
                            ST(m_s, msout[:, 1:17], ["msout"], "m_s")
                        A(I("activation", out=tmp[:], in_=tmp[:], func=AF.Exp, scale=-1.0), GK, GK)
                        transpose_rows(tmp, 2)
                        for bp in parts:
                            V(I("tensor_tensor", out=seg3(tmp, bp), in0=seg3(a_row, bp), in1=bc3(mend, bp), op=ALU.subtract),
                              GK + ["a_row", PB[5]] + [("tokm", t) for t in range(NTI)], GK)
                        A(I("activation", out=tmp[:], in_=tmp[:], func=AF.Exp), GK, GK)
                        transpose_rows(tmp, 3)
                        V(I("tensor_copy", out=tokm[:].rearrange("p a b -> p (a b)"), in_=pbs[5][:, 0:NTI * 16]),
                          [PB[5]], [("tokm", t) for t in range(NTI)] + GK)
                        dbg("tokm%d" % blk, tokm[:], [("tokm", t) for t in range(NTI)], [128, NTI, 16])
                        dbg("arow%d" % blk, a_row[:], ["a_row"], [4, NT])
                        dbg("decb%d" % blk, decb[:], ["decb"], [128, 4, 24])
                        if not GATES_OVERLAP:
                            P.barrier()
                    if stop_here("gates%d" % blk):
                        stopped = True
                        break

                    P.stage = 'b%d_s1b_heads' % blk
                    with ExitStack() as esH:
                        qT = sb(esH, "qT", [128, 2, NT], BF16)
                        kT = sb(esH, "kT", [128, 2, NT], BF16)
                        ktok = sb(esH, "ktok", [128, NTI, 256], BF16)
                        vtok = sb(esH, "vtok", [128, NTI, 258], BF16)
                        numb = sb(esH, "numb", [128, NTI, 257])
                        stats = sb(esH, "stats", [128, NTI, 6])
                        mv = sb(esH, "mv", [128, NTI, 2])
                        sc = sb(esH, "sc", [128, 8, NTI])
                        dmm = [sb(esH, "dmm%d" % i, [128, 128]) for i in range(NPM)]
                        Pm = [sb(esH, "Pm%d" % i, [128, 128]) for i in range(NPM)]
                        Sb = [sb(esH, "Sb%d" % i, [128, 128], BF16) for i in range(2)]
                        STb = [sb(esH, "STb%d" % i, [128, 128], BF16) for i in range(2)]
                        tmpn = [sb(esH, "tmpn%d" % i, [128, 257]) for i in range(2)]
                        ksc = [sb(esH, "ksc%d" % i, [128, 256], BF16) for i in range(2)]
                        osig = [sb(esH, "osig%d" % i, [128, 256]) for i in range(2)]
                        hn = [sb(esH, "hn%d" % i, [128, 256]) for i in range(2)]
                        hb = [sb(esH, "hb%d" % i, [128, 256], BF16) for i in range(2)]
                        if blk == 1:
                            NC0R = 3
                            C0f = [sb(esH, "C0f%d" % i, [128, 2, 257])[:] for i in range(NC0R)]
                            C0b = [sb(esH, "C0b%d" % i, [128, 2, 257], BF16)[:] for i in range(NC0R)]
                            for i in range(NC0X):
                                C0f.append(ucT_raw[:, i * 514:(i + 1) * 514].rearrange("p (s v) -> p s v", s=2))
                                C0b.append(mergedT_raw[:, i * 257:(i + 1) * 257].bitcast(BF16).rearrange("p (s v) -> p s v", s=2))
                            NC0 = NC0R + NC0X
                            qTm = sb(esH, "qTm", [128, 2, 8, 64], BF16)
                            kscm = sb(esH, "kscm", [64, 8, 256], BF16)
                        u_off = 3598 if blk == 1 else 0
                        m_off = 1800 if blk == 1 else 0
                        Cbf2 = ucT_raw[:, u_off:u_off + 257].bitcast(BF16).rearrange("p (s v) -> p s v", s=2)
                        osig = [mergedT_raw[:, m_off + i * 256:m_off + (i + 1) * 256] for i in range(8)] + [osig[0][:]]
                        V(I("memset", vtok[:], 1.0), [], ["vtok_init"])
                        V(I("memset", numb[:], 0.0), [], ["numb_init"])
                        V(I("memset", stats[:], 0.0), [], ["stats_init"])
                        V(I("memset", mv[:], 1.0), [], ["mv_init"])
                        if not GATES_OVERLAP:
                            P.barrier()
                        rr = [0]

                        def nxt(lst):
                            rr[0] += 1
                            return rr[0] % len(lst)

                        pending = None
                        if hstop == 0:
                            P.enabled = False
                        for h in range(NH):
                            if hstop == 0.5 and h == 0 and pending is not None:
                                P.enabled = False
                            wq, wk, wv, wo = get_items(head_items(h))
                            if h + 1 < NH:
                                w_prefetch(head_items(h + 1))
                            else:
                                w_prefetch(conv_items(0))
                            HK = ("h", h)
                            P.stage = 'b%d_s1b_h%d_inproj' % (blk, h)
                            if hstop == 0.5 and h == 0:
                                P.enabled = False
                            bi_ = 0
                            for (wt, dst, scl, nm) in ((wq, qT, 1.0, "qT"), (wk, kT, KSCALE, "kT")):
                                for sub in range(2):
                                    for (lo, n) in supers:
                                        b = bi_ % 2
                                        bi_ += 1
                                        T([I("matmul",
                                            pbs[b][:, 0:n], lhsT=wt[0][:, k, sub * 128:(sub + 1) * 128], rhs=xT[:, k, lo:lo + n],
                                            start=(k == 0), stop=(k == 7)) for k in range(8)],
                                          [wt[1]] + [("xT", t) for t in tks(lo, n)], [PB[b]])
                                        A(I("activation",
                                            out=dst[:, sub, lo:lo + n], in_=pbs[b][:, 0:n], func=AF.Copy, scale=scl),
                                          [PB[b]], [(nm, t) for t in tks(lo, n)])
                            if hstop == 1 and h == 0:
                                P.enabled = False
                            for ti, (lo, n) in enumerate(tiles):
                                b = 2 + ti % 2
                                if KTOK_TRANSPOSE:
                                    kview = pbs[b][:, 0:128].bitcast(BF16)
                                    T([I("matmul", pbs[b][0:n, 256:512], lhsT=xT[:, k, lo:lo + n], rhs=wv[0][:, k, :],
                                         start=(k == 0), stop=(k == 7)) for k in range(8)] +
                                      [I("transpose", out=kview[0:n, sub * 128:(sub + 1) * 128], in_=kT[:, sub, lo:lo + n], identity=identb[:])
                                       for sub in range(2)],
                                      [wv[1], ("xT", ti), ("kT", ti), "identb"], [PB[b]])
                                    A(I("activation", out=ktok[0:n, ti, :], in_=kview[0:n, :], func=AF.Copy),
                                      [PB[b]], [("ktok", ti)])
                                else:
                                    T([I("matmul", pbs[b][0:n, 0:256], lhsT=xT[:, k, lo:lo + n], rhs=wk[0][:, k, :],
                                         start=(k == 0), stop=(k == 7)) for k in range(8)] +
                                      [I("matmul", pbs[b][0:n, 256:512], lhsT=xT[:, k, lo:lo + n], rhs=wv[0][:, k, :],
                                         start=(k == 0), stop=(k == 7)) for k in range(8)],
                                      [wk[1], wv[1], ("xT", ti)], [PB[b]])
                                    A(I("activation", out=ktok[0:n, ti, :], in_=pbs[b][0:n, 0:256], func=AF.Copy, scale=KSCALE),
                                      [PB[b]], [("ktok", ti)])
                                (V(I("tensor_copy", out=vtok[0:n, ti, 0:256], in_=pbs[b][0:n, 256:512]),
                                   [PB[b], "vtok_init"], [("vtok", ti)]) if VTOK_ENG == "vector" else
                                 A(I("activation", out=vtok[0:n, ti, 0:256], in_=pbs[b][0:n, 256:512], func=AF.Copy),
                                   [PB[b], "vtok_init"], [("vtok", ti)]))
                            if hstop == 2 and h == 0:
                                P.enabled = False
                            P.stage = 'b%d_s1b_h%d_chunks' % (blk, h)
                            for ti, (lo, n) in enumerate(tiles):
                                if hstop == 4 and h == 0 and ti == 1:
                                    P.enabled = False
                                samp = (n == 64)
                                i2 = ti % 2
                                mk = masks[0:n, 128:192] if samp else masks[0:n, 0:128]
                                if MASK_MM:
                                    T([I("matmul", pbs[4][0:n, 0:n], lhsT=sel4[:, h, 0:n], rhs=a_row[:, lo:lo + n], start=True, stop=False),
                                       I("matmul", pbs[4][0:n, 0:n], lhsT=identf[0:n, 0:n], rhs=mk, start=False, stop=True)],
                                      ["sel4", "a_row", "identf", "masks"], [PB[4]])
                                    A(I("activation", out=Pm[ti % NPM][0:n, 0:n], in_=pbs[4][0:n, 0:n], func=AF.Exp,
                                        bias=tokm[0:n, ti, h:h + 1], scale=1.0),
                                      [PB[4], ("tokm", ti)], [("Pm", ti % NPM)])
                                else:
                                    T(I("matmul", pbs[4][0:n, 0:n], lhsT=sel4[:, h, 0:n], rhs=a_row[:, lo:lo + n], start=True, stop=True),
                                      ["sel4", "a_row"], [PB[4]])
                                    V(I("tensor_tensor", out=dmm[ti % NPM][0:n, 0:n], in0=pbs[4][0:n, 0:n], in1=mk, op=ALU.add),
                                      [PB[4], "masks"], [("dmm", ti % NPM)])
                                    A(I("activation", out=Pm[ti % NPM][0:n, 0:n], in_=dmm[ti % NPM][0:n, 0:n], func=AF.Exp,
                                        bias=tokm[0:n, ti, h:h + 1], scale=1.0),
                                      [("dmm", ti % NPM), ("tokm", ti)], [("Pm", ti % NPM)])
                                T([I("matmul", pbs[5][0:n, 0:n], lhsT=qT[:, sub, lo:lo + n], rhs=kT[:, sub, lo:lo + n],
                                                                           start=(sub == 0), stop=(sub == 1)) for sub in range(2)],
                                  [("qT", ti), ("kT", ti)], [PB[5]])
                                V(I("tensor_tensor", out=Sb[i2][0:n, 0:n], in0=pbs[5][0:n, 0:n], in1=Pm[ti % NPM][0:n, 0:n], op=ALU.mult),
                                  [PB[5], ("Pm", ti % NPM)], [("Sb", i2)])
                                T(I("transpose", out=pT[0:n, 0:n], in_=Sb[i2][0:n, 0:n], identity=identb[0:n, 0:n]),
                                  [("Sb", i2), "identb"], [KPT])
                                A(I("activation", out=STb[i2][0:n, 0:n], in_=pT[0:n, 0:n], func=AF.Copy),
                                  [KPT], [("STb", i2)])
                                T(I("matmul", pbs[6][0:n, 0:257], lhsT=STb[i2][0:n, 0:n], rhs=vtok[0:n, ti, 0:257],
                                                                        start=True, stop=True),
                                  [("STb", i2), ("vtok", ti), "vtok_init"], [PB[6]])
                                if hstop == 3 and h == 0 and ti == 0:
                                    P.enabled = False
                                G(I("tensor_scalar", out=ksc[i2][0:n, :], in0=ktok[0:n, ti, :],
                                                                               scalar1=tokm[0:n, ti, 12 + h:13 + h], scalar2=1.0, op0=ALU.mult, op1=ALU.mult),
                                  [("ktok", ti), ("tokm", ti)], [("ksc", i2)])
                                if not samp:
                                    c = ti
                                    T([I("matmul", pbs[0][0:n, 0:257], lhsT=qT[:, sub, lo:lo + n], rhs=(Cbf[:, h] if c % 2 == 0 else Cbf2)[:, sub, :],
                                                                               start=(sub == 0), stop=(sub == 1)) for sub in range(2)],
                                      [("qT", ti), (("Cbf", h) if c % 2 == 0 else "Cbf2")], [PB[0]])
                                    for sub in range(2):
                                        T(I("matmul", pbs[1 + sub][:, 0:257], lhsT=ksc[i2][0:n, sub * 128:(sub + 1) * 128],
                                                                                         rhs=vtok[0:n, ti, 0:257], start=True, stop=True),
                                          [("ksc", i2), ("vtok", ti), "vtok_init"], [PB[1 + sub]])
                                        V(I("scalar_tensor_tensor", out=Cst[:, h, sub, :], in0=Cst[:, h, sub, :],
                                                                                         scalar=decb[:, h, c:c + 1], in1=pbs[1 + sub][:, 0:257],
                                                                                         op0=ALU.mult, op1=ALU.add),
                                          [PB[1 + sub], "decb", ("Cst", h), "Cst"], [("Cst", h)])
                                    (V(I("tensor_copy", out=(Cbf[:, h] if (c + 1) % 2 == 0 else Cbf2), in_=Cst[:, h, :, :]), [("Cst", h), "gates"], [(("Cbf", h) if (c + 1) % 2 == 0 else "Cbf2")]) if CBF_ENG == "vector" else
                                     A(I("activation", out=(Cbf[:, h] if (c + 1) % 2 == 0 else Cbf2), in_=Cst[:, h, :, :], func=AF.Copy), [("Cst", h), "gates"], [(("Cbf", h) if (c + 1) % 2 == 0 else "Cbf2")]))
                                else:
                                    def fill_masked(hf):
                                        for sub in range(2):
                                            V(I("tensor_tensor", out=qTm[:, sub, :, :],
                                                in0=qT[:, sub, 1024:1088].unsqueeze(1).to_broadcast([128, 8, 64]),
                                                in1=bmask[:, hf * 8:(hf + 1) * 8, :], op=ALU.mult),
                                              [("qT", ti), "bmask"], ["qTm"])
                                        V(I("tensor_tensor", out=kscm[:], in0=ksc[i2][0:64, :].unsqueeze(1).to_broadcast([64, 8, 256]),
                                            in1=rmask[:, hf * 8:(hf + 1) * 8].unsqueeze(2).to_broadcast([64, 8, 256]), op=ALU.mult),
                                          [("ksc", i2), "rmask"], ["kscm"])
                                    for bq in range(NSEQ):
                                        if bq % 8 == 0:
                                            fill_masked(bq // 8)
                                        cs = (h * NSEQ + bq) % NC0
                                        LD(C0f[cs][:, :, 0:256], sC[bq, h].rearrange("(s p) v -> p s v", p=128), [("C0f", cs)], ("C0f", cs), r=["gates"])
                                        G(I("tensor_copy", out=C0f[cs][:, :, 256:257], in_=snT_t[:, :, bq, h:h + 1]),
                                          ["snT", "gates"], [("C0n", cs)], )
                                        A(I("activation", out=C0b[cs][:], in_=C0f[cs][:], func=AF.Copy), [("C0f", cs), ("C0n", cs), "gates"], [("C0b", cs)])
                                        T([I("matmul", pbs[0][0:64, 0:257], lhsT=qTm[:, sub, bq % 8, :], rhs=C0b[cs][:, sub, :],
                                                                                     start=(bq == 0 and sub == 0), stop=(bq == NSEQ - 1 and sub == 1))
                                           for sub in range(2)],
                                          ["qTm", ("C0b", cs)], [PB[0]])
                                        for sub in range(2):
                                            T(I("matmul", pbs[1 + sub][:, 0:257], lhsT=kscm[:, bq % 8, sub * 128:(sub + 1) * 128],
                                                                                        rhs=vtok[0:64, ti, 0:257], start=True, stop=True),
                                              ["kscm", ("vtok", ti), "vtok_init"], [PB[1 + sub]])
                                            V(I("scalar_tensor_tensor", out=C0f[cs][:, sub, :], in0=C0f[cs][:, sub, :],
                                                                                                      scalar=decb[:, h, 8 + bq:9 + bq], in1=pbs[1 + sub][:, 0:257],
                                                                                                      op0=ALU.mult, op1=ALU.add),
                                              [PB[1 + sub], "decb", ("C0f", cs), ("C0n", cs), ("C0b", cs)], [("C0f", cs), ("C0n", cs)])
                                        ST(C_s[bq, h].rearrange("(s p) v -> p s v", p=128), C0f[cs][:, :, 0:256], [("C0f", cs)], ("C0f", cs), eng=CS_ST_ENG)
                                        G(I("tensor_copy", out=nsout[:, bq, h, :], in_=C0f[cs][:, :, 256]),
                                          [("C0n", cs), ("C0f", cs)], ["nsout"])
                                A(I("activation", out=tmpn[i2][0:n, :], in_=pbs[0][0:n, 0:257], func=AF.Identity,
                                                                            scale=tokm[0:n, ti, 4 + h:5 + h], bias=0.0),
                                  [PB[0], ("tokm", ti)], [("tmpn", i2)])
                                V(I("tensor_tensor", out=numb[0:n, ti, :], in0=tmpn[i2][0:n, :], in1=pbs[6][0:n, 0:257], op=ALU.add),
                                  [("tmpn", i2), PB[6], "numb_init"], [("numb", ti)])
                                V(I("bn_stats", out=stats[0:n, ti, :], in_=numb[0:n, ti, 0:256]),
                                  [("numb", ti), "stats_init"], [("stats", ti)])
                            if hstop == 5 and h == 0:
                                P.enabled = False
                            P.stage = 'b%d_s1b_h%d_phaseB' % (blk, h)
                            allnum = [("numb", t) for t in range(NTI)] + [("stats", t) for t in range(NTI)] + [("tokm", t) for t in range(NTI)]
                            V([I("bn_aggr", out=mv[0:n, ti, :], in_=stats[0:n, ti, :]) for ti, (lo, n) in enumerate(tiles)], allnum + ["mv_init"], ["mv"])
                            regions = [(slice(0, 128), slice(0, 8))] + ([(slice(0, 64), slice(8, 9))] if blk == 1 else [])
                            for ri, (rw, ts_) in enumerate(regions):
                                den = numb[rw, ts_, 256]
                                SK = lambda i: ("sc", i, ri)
                                V(I("scalar_tensor_tensor", out=sc[rw, 0, ts_], in0=den, scalar=-1.0, in1=den, op0=ALU.mult, op1=ALU.max), allnum, [SK(0)])
                                V(I("tensor_tensor", out=sc[rw, 1, ts_], in0=sc[rw, 0, ts_], in1=tokm[rw, ts_, 8 + h], op=ALU.max), [SK(0)] + allnum, [SK(1)])
                                V(I("reciprocal", out=sc[rw, 2, ts_], in_=sc[rw, 1, ts_]), [SK(1)], [SK(2)])
                                V(I("tensor_tensor", out=sc[rw, 3, ts_], in0=sc[rw, 2, ts_], in1=sc[rw, 2, ts_], op=ALU.mult), [SK(2)], [SK(3)])
                                V(I("tensor_tensor", out=sc[rw, 4, ts_], in0=sc[rw, 3, ts_], in1=mv[rw, ts_, 1], op=ALU.mult), [SK(3), "mv"], [SK(4)])
                                V(I("tensor_scalar", out=sc[rw, 4, ts_], in0=sc[rw, 4, ts_], scalar1=LN_EPS, scalar2=None, op0=ALU.add), [SK(4)], [SK(4)])
                                A(I("activation", out=sc[rw, 5, ts_], in_=sc[rw, 4, ts_], func=AF.Sqrt), [SK(4)], [SK(5)])
                                V(I("reciprocal", out=sc[rw, 6, ts_], in_=sc[rw, 5, ts_]), [SK(5)], [SK(6)])
                                V(I("tensor_tensor", out=sc[rw, 7, ts_], in0=sc[rw, 6, ts_], in1=sc[rw, 2, ts_], op=ALU.mult), [SK(6), SK(2)], ["sc7"])
                            for ti, (lo, n) in enumerate(tiles):
                                i2 = ti % 2
                                T([I("matmul", pbs[3][0:n, 0:256], lhsT=xT[:, k, lo:lo + n], rhs=wo[0][:, k, :],
                                                                       start=(k == 0), stop=(k == 7)) for k in range(8)],
                                  [wo[1], ("xT", ti)], [PB[3]])
                                A(I("activation", out=osig[ti][0:n, :], in_=pbs[3][0:n, 0:256], func=AF.Sigmoid),
                                  [PB[3], "gates"], [("osig", ti)])
                                if HB_MODE == "fuse_act":
                                    V(I("scalar_tensor_tensor", out=hn[i2][0:n, :], in0=numb[0:n, ti, 0:256], scalar=mv[0:n, ti, 0:1],
                                        in1=osig[ti][0:n, :], op0=ALU.subtract, op1=ALU.mult),
                                      [("numb", ti), "mv", ("osig", ti)], [("hn", i2)])
                                    A(I("activation", out=hb[i2][0:n, :], in_=hn[i2][0:n, :], func=AF.Copy, scale=sc[0:n, 7, ti:ti + 1]),
                                      [("hn", i2), "sc7"], [("hb", i2)])
                                else:
                                    V(I("tensor_scalar", out=hn[i2][0:n, :], in0=numb[0:n, ti, 0:256], scalar1=mv[0:n, ti, 0:1],
                                        scalar2=sc[0:n, 7, ti:ti + 1], op0=ALU.subtract, op1=ALU.mult),
                                      [("numb", ti), "mv", "sc7"], [("hn", i2)])
                                    P.op("gpsimd" if HB_MODE == "pool" else "vector",
                                         I("tensor_tensor", out=hb[i2][0:n, :], in0=hn[i2][0:n, :], in1=osig[ti][0:n, :], op=ALU.mult),
                                         [("hn", i2), ("osig", ti)], [("hb", i2)])
                                T([I("transpose", out=pT[:, sub * 128: sub * 128 + n], in_=hb[i2][0:n, sub * 128:(sub + 1) * 128],
                                                                              identity=identb[0:n, 0:n]) for sub in range(2)],
                                  [("hb", i2), "identb"], [KPT])
                                A(I("activation", out=hnT[:, 2 * h:2 * h + 2, lo:lo + n],
                                                                     in_=pT[:, 0:256].rearrange("p (s t) -> p s t", t=128)[:, :, 0:n], func=AF.Copy),
                                  [KPT], [("hnT", h, ti)])
                            w_free(wq[2], wk[2], wv[2], wo[2])
                            if hstop == 6 and h == 0:
                                P.enabled = False
                        if blk == 1:
                            for h in range(NH):
                                ST(C_p[h].rearrange("(s p) v -> p s v", p=128), Cst[:, h, :, 0:256], [("Cst", h)], ("Cp", h))
                            G(I("tensor_copy", out=npout[:], in_=Cst[:, :, :, 256]), [("Cst", h) for h in range(NH)], ["npout"])
                            ST(n_p, npout[:], ["npout"], "npout")
                            ST(n_s, nsout[:], ["nsout"], "nsout")
                        dbg("hnT%d" % blk, hnT[:], [("hnT", h, t) for h in range(NH) for t in range(NTI)], [128, 8, NT])
                        dbg("qT%d" % blk, qT[:], [("qT", t) for t in range(NTI)], [128, 2, NT])
                        dbg("numb%d" % blk, numb[:], [("numb", t) for t in range(NTI)], [128, NTI, 257])
                        P.barrier()
                    if stop_here("heads%d" % blk):
                        stopped = True
                        break

                    P.stage = 'b%d_s1c_conv' % blk
                    with ExitStack() as esC:
                        ubuf = [sb(esC, "ubuf%d" % i, [128, 1026]) for i in range(2)]
                        usmp = [sb(esC, "usmp%d" % i, [128, NSEQ, 6]) for i in range(2)]
                        cgs = [sb(esC, "cgs%d" % i, [128, 512]) for i in range(2)]
                        ac0 = [sb(esC, "ac0%d" % i, [128, 512]) for i in range(2)]
                        ac1 = [sb(esC, "ac1%d" % i, [128, 512]) for i in range(2)]
                        P.barrier()
                        pend = None
                        it = 0
                        for j in range(4):
                            wbg, wcg, whc = get_items(conv_items(j))
                            if j < 3:
                                w_prefetch(conv_items(j + 1))
                            else:
                                w_prefetch(merge_items(0))
                            for sub in range(2):
                                cb = 2 * j + sub
                                ub = ubuf[cb % 2]
                                us = usmp[cb % 2]
                                UK = ("ubuf", cb % 2)
                                V(I("tensor_copy", out=ub[:, 0:2], in_=convcar[:, cb, :]), ["convcar", UK], [UK])
                                if blk == 1:
                                    V(I("tensor_copy", out=us[:, :, 0:2], in_=scv[:, cb, :, :]), ["scv", UK], [UK])
                                for (lo, n) in supers:
                                    samp = (n == 64)
                                    s_ = it % 2
                                    it += 1
                                    bb = (0, 1, 2) if s_ == 0 else (3, 4, 5)
                                    for wt, b in ((wcg, bb[0]), (whc, bb[1]), (wbg, bb[2])):
                                        T([I("matmul",
                                            pbs[b][:, 0:n], lhsT=wt[0][:, k, sub * 128:(sub + 1) * 128], rhs=xT[:, k, lo:lo + n],
                                            start=(k == 0), stop=(k == 7)) for k in range(8)],
                                          [wt[1]] + [("xT", t) for t in tks(lo, n)], [PB[b]])
                                    A(I("activation", out=cgs[s_][:, 0:n], in_=pbs[bb[0]][:, 0:n], func=AF.Copy),
                                      [PB[bb[0]]], [("cgs", s_)])
                                    if not samp:
                                        V(I("tensor_tensor", out=ub[:, 2 + lo:2 + lo + n], in0=pbs[bb[1]][:, 0:n],
                                                                                                     in1=cgs[s_][:, 0:n], op=ALU.mult),
                                          [PB[bb[1]], ("cgs", s_), UK], [UK])
                                        t0 = lambda o, ub=ub, lo=lo, n=n: ub[:, lo + o:lo + o + n]
                                        a0 = ac0[s_][:, 0:n]
                                        a1 = ac1[s_][:, 0:n]
                                        bgp = pbs[bb[2]][:, 0:n]
                                        dst = ucT[:, cb, lo:lo + n]
                                    else:
                                        V(I("tensor_tensor", out=us[:, :, 2:6], in0=pbs[bb[1]][:, 0:64].rearrange("p (b t) -> p b t", t=4),
                                                                                         in1=cgs[s_][:, 0:64].rearrange("p (b t) -> p b t", t=4), op=ALU.mult),
                                          [PB[bb[1]], ("cgs", s_), UK], [UK])
                                        t0 = lambda o, us=us: us[:, :, o:o + 4]
                                        a0 = ac0[s_][:, 0:64].rearrange("p (b t) -> p b t", t=4)
                                        a1 = ac1[s_][:, 0:64].rearrange("p (b t) -> p b t", t=4)
                                        bgp = pbs[bb[2]][:, 0:64].rearrange("p (b t) -> p b t", t=4)
                                        dst = ucT[:, cb, 1024:1088].rearrange("p (b t) -> p b t", t=4)
                                    G(I("tensor_scalar", out=a0, in0=t0(0), scalar1=cw[:, cb, 0:1], scalar2=1.0, op0=ALU.mult, op1=ALU.mult),
                                      [UK, "cw"], [("ac0", s_)])
                                    V(I("scalar_tensor_tensor", out=a1, in0=t0(1), scalar=cw[:, cb, 1:2], in1=a0,
                                                                                                  op0=ALU.mult, op1=ALU.add),
                                      [UK, "cw", ("ac0", s_)], [("ac1", s_)])
                                    V(I("scalar_tensor_tensor", out=a0, in0=t0(2), scalar=cw[:, cb, 2:3], in1=a1,
                                                                                                  op0=ALU.mult, op1=ALU.add),
                                      [UK, "cw", ("ac1", s_)], [("ac0", s_)])
                                    V(I("tensor_tensor", out=dst, in0=bgp, in1=a0, op=ALU.mult),
                                      [PB[bb[2]], ("ac0", s_)], [("ucT", cb, t) for t in tks(lo, n)])
                                if blk == 0:
                                    V(I("tensor_copy", out=convcar[:, cb, :], in_=ub[:, 1024:1026]), [UK, "convcar"], ["convcar"])
                                else:
                                    V(I("tensor_copy", out=convout[:, cb, :], in_=ub[:, 1024:1026]), [UK], ["convout"])
                                    V(I("tensor_copy", out=convsout[:, cb, :, :], in_=us[:, :, 4:6]), [UK], ["convsout"])
                            w_free(wbg[2], wcg[2], whc[2])
                        if blk == 1:
                            ST(conv_p, convout[:], ["convout"], "convout")
                            ST(conv_s, convsout[:], ["convsout"], "convsout")
                        dbg("ucT%d" % blk, ucT[:], [("ucT", cb, t) for cb in range(8) for t in range(NTI)], [128, 8, NT])
                        P.barrier()
                    if stop_here("conv%d" % blk):
                        stopped = True
                        break

                    P.stage = 'b%d_s2_merge' % blk
                    with ExitStack() as es2:
                        sg = [sb(es2, "sg%d" % i, [128, 512]) for i in range(4)]
                        t12 = [sb(es2, "t12%d" % i, [128, 512]) for i in range(4)]
                        P.barrier()
                        pend = None
                        it = 0
                        for j in range(4):
                            wc, wm, wgc, wgm = get_items(merge_items(j))
                            if j < 3:
                                w_prefetch(merge_items(j + 1))
                            else:
                                w_prefetch(wo_items())
                            for sub in range(2):
                                fb = 2 * j + sub
                                for (lo, n) in supers:
                                    s_ = it % 2
                                    it += 1
                                    bb = (0, 1, 2, 3) if s_ == 0 else (4, 5, 6, 7)
                                    srcs = ((wc, ucT, "ucT", bb[0]), (wm, hnT, "hnT", bb[1]), (wgc, xT, "xT", bb[2]), (wgm, xT, "xT", bb[3]))
                                    for wt, src, nm, b in srcs:
                                        if nm == "ucT":
                                            rk = [("ucT", cb, t) for cb in range(8) for t in tks(lo, n)]
                                        elif nm == "hnT":
                                            rk = [("hnT", hh, t) for hh in range(NH) for t in tks(lo, n)]
                                        else:
                                            rk = [("xT", t) for t in tks(lo, n)]
                                        T([I("matmul",
                                            pbs[b][:, 0:n], lhsT=wt[0][:, k, sub * 128:(sub + 1) * 128], rhs=src[:, k, lo:lo + n],
                                            start=(k == 0), stop=(k == 7)) for k in range(8)],
                                          [wt[1]] + rk, [PB[b]])
                                    A(I("activation", out=sg[2 * s_][:, 0:n], in_=pbs[bb[2]][:, 0:n], func=AF.Sigmoid),
                                      [PB[bb[2]]], [("sg", 2 * s_)])
                                    A(I("activation", out=sg[2 * s_ + 1][:, 0:n], in_=pbs[bb[3]][:, 0:n], func=AF.Sigmoid),
                                      [PB[bb[3]]], [("sg", 2 * s_ + 1)])
                                    V(I("tensor_tensor", out=t12[2 * s_][:, 0:n], in0=pbs[bb[0]][:, 0:n], in1=sg[2 * s_][:, 0:n], op=ALU.mult),
                                      [PB[bb[0]], ("sg", 2 * s_)], [("t12", 2 * s_)])
                                    V(I("tensor_tensor", out=t12[2 * s_ + 1][:, 0:n], in0=pbs[bb[1]][:, 0:n], in1=sg[2 * s_ + 1][:, 0:n], op=ALU.mult),
                                      [PB[bb[1]], ("sg", 2 * s_ + 1)], [("t12", 2 * s_ + 1)])
                                    G(I("tensor_tensor", out=mergedT[:, fb, lo:lo + n], in0=t12[2 * s_][:, 0:n],
                                                                                          in1=t12[2 * s_ + 1][:, 0:n], op=ALU.add),
                                      [("t12", 2 * s_), ("t12", 2 * s_ + 1)], [("mT", fb, t) for t in tks(lo, n)])
                            w_free(wc[2], wm[2], wgc[2], wgm[2])
                        dbg("mT%d" % blk, mergedT[:], [("mT", fb, t) for fb in range(8) for t in range(NTI)], [128, 8, NT])
                        P.barrier()
                if stopped or stop_here("merge%d" % blk):
                    stopped = True
                    break

                P.stage = 'b%d_s3_wo' % blk
                with ExitStack() as esB:
                    x1T = sb(esB, "x1T%d" % blk, [128, 8, NT], BF16)
                    acc = sb(esB, "acc%d" % blk, [128, NTI, D])
                    lng = sb(esB, "lng", [128, D])
                    lnb = sb(esB, "lnb", [128, D])
                    st2 = sb(esB, "st2", [128, 3, 2, 6])
                    mv2 = sb(esB, "mv2", [128, 3, 4])
                    with ExitStack() as es3:
                        xt = [sb(es3, "xt%d" % i, [128, D]) for i in range(NB3)]
                        s1 = [sb(es3, "s1%d" % i, [128, D]) for i in range(NB3)]
                        x1b = [sb(es3, "x1b%d" % i, [128, D], BF16) for i in range(NB3)]
                        P.barrier()
                        LD(lng[:], ln1g.partition_broadcast(128), ["lng"], "lng")
                        LD(lnb[:], ln1b.partition_broadcast(128), ["lnb"], "lnb")
                        wo4 = list(get_items(wo_items()))
                        w_prefetch(ffn_items(0))
                        for ti, (lo, n) in enumerate(tiles):
                            i2 = ti % NB3
                            bb = ((0, 1), (2, 3), (4, 5))[ti % 3]
                            src_x = x_p[blk * 1024 + lo: blk * 1024 + lo + n, :] if n == 128 else x_s
                            LD(xt[i2][0:n, :], src_x, [("xt", i2)], ("xt", i2))
                            for half in range(2):
                                for q4 in range(2):
                                    wt = wo4[2 * half + q4]
                                    T([I("matmul",
                                        pbs[bb[half]][0:n, q4 * 256:(q4 + 1) * 256], lhsT=mergedT[:, k, lo:lo + n], rhs=wt[0][:, k, :],
                                        start=(k == 0), stop=(k == 7)) for k in range(8)],
                                      [wt[1]] + [("mT", fb, ti) for fb in range(8)], [PB[bb[half]]])
                                V(I("scalar_tensor_tensor",
                                    out=s1[i2][0:n, half * 512:(half + 1) * 512], in0=xt[i2][0:n, half * 512:(half + 1) * 512], scalar=ALPHA,
                                    in1=pbs[bb[half]][0:n, :], op0=ALU.mult, op1=ALU.add),
                                  [("xt", i2), PB[bb[half]], ("s1", i2)], [("s1", i2)])
                                V(I("bn_stats", out=st2[0:n, i2, half, :], in_=s1[i2][0:n, half * 512:(half + 1) * 512]),
                                  [("s1", i2)], [("st2", i2, half)])
                            V(I("bn_aggr", out=mv2[0:n, i2, 0:2], in_=st2[0:n, i2, :, :].rearrange("p a b -> p (a b)")),
                              [("st2", i2, 0), ("st2", i2, 1)], [("mv2", i2)])
                            V(I("tensor_scalar", out=mv2[0:n, i2, 2:3], in0=mv2[0:n, i2, 1:2], scalar1=LN_EPS, scalar2=None, op0=ALU.add),
                              [("mv2", i2)], [("mv2", i2)])
                            A(I("activation", out=mv2[0:n, i2, 2:3], in_=mv2[0:n, i2, 2:3], func=AF.Sqrt), [("mv2", i2)], [("mv2", i2)])
                            V(I("reciprocal", out=mv2[0:n, i2, 3:4], in_=mv2[0:n, i2, 2:3]), [("mv2", i2)], [("mv2", i2)])
                            if LN_FUSE:
                                V(I("scalar_tensor_tensor", out=s1[i2][0:n, :], in0=s1[i2][0:n, :], scalar=mv2[0:n, i2, 0:1], in1=lng[0:n, :],
                                    op0=ALU.subtract, op1=ALU.mult), [("s1", i2), ("mv2", i2), "lng"], [("s1", i2)])
                                V(I("scalar_tensor_tensor", out=s1[i2][0:n, :], in0=s1[i2][0:n, :], scalar=mv2[0:n, i2, 3:4], in1=lnb[0:n, :],
                                    op0=ALU.mult, op1=ALU.add), [("s1", i2), ("mv2", i2), "lnb"], [("s1", i2)])
                            else:
                                V(I("tensor_scalar", out=s1[i2][0:n, :], in0=s1[i2][0:n, :], scalar1=mv2[0:n, i2, 0:1], scalar2=mv2[0:n, i2, 3:4],
                                                                        op0=ALU.subtract, op1=ALU.mult),
                                  [("s1", i2), ("mv2", i2)], [("s1", i2)])
                                V(I("tensor_tensor", out=s1[i2][0:n, :], in0=s1[i2][0:n, :], in1=lng[0:n, :], op=ALU.mult),
                                  [("s1", i2), "lng"], [("s1", i2)])
                                P.op(LN1B_ENG[ti % len(LN1B_ENG)], I("tensor_tensor", out=s1[i2][0:n, :], in0=s1[i2][0:n, :], in1=lnb[0:n, :], op=ALU.add),
                                     [("s1", i2), "lnb"], [("s1", i2)])
                            if ACC_ENG == "scalar":
                                A(I("activation", out=acc[0:n, ti, :], in_=s1[i2][0:n, :], func=AF.Copy, scale=ALPHA),
                                  [("s1", i2)], [("acc", ti)])
                            else:
                                G(I("tensor_scalar", out=acc[0:n, ti, :], in0=s1[i2][0:n, :], scalar1=ALPHA, scalar2=1.0, op0=ALU.mult, op1=ALU.mult),
                                  [("s1", i2)], [("acc", ti)])
                            A(I("activation", out=x1b[i2][0:n, :], in_=s1[i2][0:n, :], func=AF.Copy), [("s1", i2)], [("x1b", i2)])
                            T([I("transpose", out=pT[:, k * 128:k * 128 + n], in_=x1b[i2][0:n, k * 128:(k + 1) * 128],
                                                                      identity=identb[0:n, 0:n]) for k in range(8)],
                              [("x1b", i2), "identb"], [KPT])
                            if X1T_ENG == "vector":
                                V(I("tensor_copy", out=x1T[:, :, lo:lo + n], in_=pT[:].rearrange("p (k t) -> p k t", t=128)[:, :, 0:n]),
                                  [KPT], [("x1T", ti)])
                            else:
                                A(I("activation", out=x1T[:, :, lo:lo + n], in_=pT[:].rearrange("p (k t) -> p k t", t=128)[:, :, 0:n], func=AF.Copy),
                                  [KPT], [("x1T", ti)])
                        w_free(*[w[2] for w in wo4])
                        dbg("x1T%d" % blk, x1T[:], [("x1T", t) for t in range(NTI)], [128, 8, NT])
                        dbg("acc%d" % blk, acc[:], [("acc", t) for t in range(NTI)], [128, NTI, D])
                        P.barrier()
                    if stop_here("wo%d" % blk):
                        stopped = True
                        break
                    P.stage = 'b%d_s4_ffn' % blk
                    with ExitStack() as es4:
                        hidT = [sb(es4, "hidT%d" % i, [128, 4, FF_GRP], BF16) for i in range(NHID)]
                        rl = [sb(es4, "rl%d" % i, [128, FF_GRP]) for i in range(4)]
                        yt = [sb(es4, "yt%d" % i, [128, D]) for i in range(NYT)]
                        P.barrier()
                        LD(lng[:], ln2g.partition_broadcast(128), ["lng"], "lng")
                        LD(lnb[:], ln2b.partition_broadcast(128), ["lnb"], "lnb")
                        pairs = [(i * FF_GRP, FF_GRP) for i in range(1024 // FF_GRP)] + ([(1024, 64)] if blk == 1 else [])
                        pend = None
                        cnt = 0
                        for g in range(8):
                            w1a, w1b, w2a, w2b = get_items(ffn_items(g))
                            if g < 7:
                                w_prefetch(ffn_items(g + 1))
                            elif blk == 0:
                                w_prefetch(head_items(0))
                            for pi, (lo, n) in enumerate(pairs):
                                hd = hidT[(g * 5 + pi) % NHID]
                                HKY = ("hidT", (g * 5 + pi) % NHID)
                                for hs in range(4):
                                    wt = w1a if hs < 2 else w1b
                                    b = cnt % 4
                                    cnt += 1
                                    T([I("matmul",
                                        pbs[b][:, 0:n], lhsT=wt[0][:, k, (hs % 2) * 128:(hs % 2 + 1) * 128], rhs=x1T[:, k, lo:lo + n],
                                        start=(k == 0), stop=(k == 7)) for k in range(8)],
                                      [wt[1]] + [("x1T", t) for t in tks(lo, n)], [PB[b]])
                                    A(I("activation", out=rl[b][:, 0:n], in_=pbs[b][:, 0:n], func=AF.Relu), [PB[b]], [("rl", b)])
                                    P.op(SQ_ENG[cnt % len(SQ_ENG)], I("tensor_tensor", out=hd[:, hs, 0:n], in0=rl[b][:, 0:n], in1=rl[b][:, 0:n], op=ALU.mult),
                                         [("rl", b)], [HKY])
                                for tj, ti in enumerate(tks(lo, n)):
                                    tn = min(128, n)
                                    for half in range(2):
                                        wt = w2a if half == 0 else w2b
                                        b = 4 + (tj % 2) * 2 + half
                                        T([I("matmul",
                                            pbs[b][0:tn, :], lhsT=hd[:, hs, tj * 128:tj * 128 + tn], rhs=wt[0][:, hs, :],
                                            start=(hs == 0), stop=(hs == 3)) for hs in range(4)],
                                          [wt[1], HKY], [PB[b]])
                                        V(I("tensor_tensor", out=acc[0:tn, ti, half * 512:(half + 1) * 512],
                                                                                                  in0=acc[0:tn, ti, half * 512:(half + 1) * 512],
                                                                                                  in1=pbs[b][0:tn, :], op=ALU.add),
                                          [PB[b], ("acc", ti)], [("acc", ti)])
                            w_free(w1a[2], w1b[2], w2a[2], w2b[2])
                        for ti, (lo, n) in enumerate(tiles):
                            i2 = ti % NYT
                            for half in range(2):
                                V(I("bn_stats", out=st2[0:n, i2, half, :], in_=acc[0:n, ti, half * 512:(half + 1) * 512]),
                                  [("acc", ti)], [("st2", i2, half)])
                            V(I("bn_aggr", out=mv2[0:n, i2, 0:2], in_=st2[0:n, i2, :, :].rearrange("p a b -> p (a b)")),
                              [("st2", i2, 0), ("st2", i2, 1)], [("mv2", i2)])
                            V(I("tensor_scalar", out=mv2[0:n, i2, 2:3], in0=mv2[0:n, i2, 1:2], scalar1=LN_EPS, scalar2=None, op0=ALU.add),
                              [("mv2", i2)], [("mv2", i2)])
                            A(I("activation", out=mv2[0:n, i2, 2:3], in_=mv2[0:n, i2, 2:3], func=AF.Sqrt), [("mv2", i2)], [("mv2", i2)])
                            V(I("reciprocal", out=mv2[0:n, i2, 3:4], in_=mv2[0:n, i2, 2:3]), [("mv2", i2)], [("mv2", i2)])
                            if LN_FUSE:
                                V(I("scalar_tensor_tensor", out=yt[i2][0:n, :], in0=acc[0:n, ti, :], scalar=mv2[0:n, i2, 0:1], in1=lng[0:n, :],
                                    op0=ALU.subtract, op1=ALU.mult), [("acc", ti), ("mv2", i2), ("yt", i2), "lng"], [("yt", i2)])
                                V(I("scalar_tensor_tensor", out=yt[i2][0:n, :], in0=yt[i2][0:n, :], scalar=mv2[0:n, i2, 3:4], in1=lnb[0:n, :],
                                    op0=ALU.mult, op1=ALU.add), [("yt", i2), ("mv2", i2), "lnb"], [("yt", i2)])
                            else:
                                V(I("tensor_scalar", out=yt[i2][0:n, :], in0=acc[0:n, ti, :], scalar1=mv2[0:n, i2, 0:1],
                                                                               scalar2=mv2[0:n, i2, 3:4], op0=ALU.subtract, op1=ALU.mult),
                                  [("acc", ti), ("mv2", i2), ("yt", i2)], [("yt", i2)])
                                P.op(LN2G_ENG, I("tensor_tensor", out=yt[i2][0:n, :], in0=yt[i2][0:n, :], in1=lng[0:n, :], op=ALU.mult),
                                     [("yt", i2), "lng"], [("yt", i2)])
                                P.op(LN2B_ENG[ti % len(LN2B_ENG)], I("tensor_tensor", out=yt[i2][0:n, :], in0=yt[i2][0:n, :], in1=lnb[0:n, :], op=ALU.add),
                                     [("yt", i2), "lnb"], [("yt", i2)])
                            dsty = y_p[blk * 1024 + lo: blk * 1024 + lo + n, :] if n == 128 else y_s
                            ST(dsty, yt[i2][0:n, :], [("yt", i2)], ("yt", i2))
                        P.barrier()

        P.emit(lambda name: top.enter_context(nc.semaphore(name)))
    return nc, dbg_outs, P


def _consts():
    ident = np.eye(128, dtype=np.float32)
    masks = np.zeros((128, 192), np.float32)
    t = np.arange(128)[:, None]
    s = np.arange(128)[None, :]
    masks[:, 0:128] = np.where(s <= t, 0.0, NEG)
    t6 = np.arange(64)[:, None]
    s6 = np.arange(64)[None, :]
    bd = np.where((s6 // 4 == t6 // 4) & (s6 <= t6), 0.0, NEG)
    masks[0:64, 128:192] = bd
    sel4 = np.zeros((4, 4, 128), np.float32)
    selm = np.zeros((4, 4, 16), np.float32)
    for k in range(4):
        sel4[k, k, :] = 1.0
        selm[k, k, :] = 1.0
    bmask = np.zeros((128, 16, 64), np.float32)
    for b in range(16):
        bmask[:, b, 4 * b:4 * b + 4] = 1.0
    rmask = np.zeros((64, 16), np.float32)
    for p in range(64):
        rmask[p, p // 4] = 1.0
    r01 = np.zeros((4, 2, 64), np.float32)
    r01[:, 0, :] = 1.0
    r01[:, 0, 0::4] = 0.0
    r01[:, 1, 0::4] = -1.0e30
    return dict(c_ident=ident, c_masks=masks, c_sel4=sel4, c_selm=selm, c_bmask=bmask, c_rmask=rmask, c_r01=r01)


def _wblocks(w, nblk):
    return np.ascontiguousarray(w.reshape(8, 128, nblk, 256).transpose(2, 1, 0, 3))


def prep_inputs(x_prompt, x_sample, state_conv, state_C, state_n, state_m, w_in, b_gate, conv_w,
                w_conv_out, mh_g, w_m_out, w_o, ln1_g, ln1_b, w_ff1, w_ff2, ln2_g, ln2_b):
    f = lambda a: np.ascontiguousarray(np.asarray(a, dtype=np.float32))
    w_in0 = f(w_in)[0]
    w_main = np.concatenate([w_in0[:, :7168], w_in0[:, 7176:]], axis=1)
    shared = dict(
        w_in_l=_wblocks(w_main, 36),
        w_g=f(w_in0[:, 7168:7176].reshape(8, 128, 8).transpose(1, 0, 2)),
        bgate=f(f(b_gate)[0].reshape(2, 4).T),
        convw=f(f(conv_w)[0].reshape(3, 8, 128).transpose(2, 1, 0)),
        wco_l=_wblocks(f(w_conv_out)[0], 4),
        mhg_l=f(f(mh_g)[0].reshape(8, 128).T),
        wmo_l=_wblocks(f(w_m_out)[0], 4),
        wo_l=_wblocks(f(w_o)[0], 4),
        ln1g=f(ln1_g).reshape(1, D), ln1b=f(ln1_b).reshape(1, D),
        wff1_l=_wblocks(f(w_ff1)[0], 16),
        wff2_l=f(f(w_ff2)[0].reshape(8, 4, 128, 2, 512).transpose(0, 3, 2, 1, 4)),
        ln2g=f(ln2_g).reshape(1, D), ln2b=f(ln2_b).reshape(1, D),
    )
    shared.update(_consts())
    xp = f(x_prompt)
    xs = f(x_sample)
    sc = f(state_conv)[0]
    sCc = f(state_C)[0]
    sn = f(state_n)[0]
    sm = f(state_m)[0]
    in_maps = []
    for c in range(NCORE):
        sl = slice(c * NSEQ, (c + 1) * NSEQ)
        xsc = xs[sl].reshape(64, D)
        m = dict(shared)
        m.update(
            xT_p=f(xp[c].T.reshape(8, 128, SEQ).transpose(1, 0, 2)),
            x_p=xp[c],
            xT_s=f(xsc.T.reshape(8, 128, 64).transpose(1, 0, 2)),
            x_s=xsc,
            sconvT=f(sc[sl].reshape(NSEQ, 2, 8, 128).transpose(3, 2, 0, 1)),
            sC=sCc[sl],
            snT=f(sn[sl].reshape(NSEQ, NH, 2, 128).transpose(3, 2, 0, 1)),
            smT=f(sm[sl].T),
        )
        in_maps.append(m)
    return in_maps


_NC_CACHE = {}


def kernel(**inputs):
    in_maps = prep_inputs(**inputs)
    if "nc" not in _NC_CACHE:
        _NC_CACHE["nc"] = build_nc()[0]
    nc = _NC_CACHE["nc"]
    res = run_bass_kernel_spmd(nc, in_maps, core_ids=list(range(NCORE)))
    R = res.results
    y_prompt = np.stack([R[c]["y_p"] for c in range(NCORE)], 0)
    y_sample = np.concatenate([R[c]["y_s"].reshape(NSEQ, DEC, D) for c in range(NCORE)], 0)
    conv_prompt = np.stack([R[c]["conv_p"].transpose(2, 1, 0).reshape(2, D) for c in range(NCORE)], 0)[None]
    conv_sample = np.concatenate([R[c]["conv_s"].transpose(2, 3, 1, 0).reshape(NSEQ, 2, D) for c in range(NCORE)], 0)[None]
    C_prompt = np.stack([R[c]["C_p"] for c in range(NCORE)], 0)[None]
    C_sample = np.concatenate([R[c]["C_s"] for c in range(NCORE)], 0)[None]
    n_prompt = np.stack([R[c]["n_p"].transpose(1, 2, 0).reshape(NH, DH) for c in range(NCORE)], 0)[None]
    n_sample = np.concatenate([R[c]["n_s"].transpose(1, 2, 3, 0).reshape(NSEQ, NH, DH) for c in range(NCORE)], 0)[None]
    m_prompt = np.stack([R[c]["m_p"].reshape(NH) for c in range(NCORE)], 0)[None]
    m_sample = np.concatenate([R[c]["m_s"].T for c in range(NCORE)], 0)[None]
    outs = (y_prompt, y_sample, conv_prompt, conv_sample, C_prompt, C_sample, n_prompt, n_sample, m_prompt, m_sample)
    return tuple(np.ascontiguousarray(o, dtype=np.float32) for o in outs)
```
